# Optimizing a Trainium2 kernel written in Bass

```python
import math
import jax, jax.numpy as jnp
from jax import lax
import numpy as np

D_MODEL = 1024
BATCH = 8
SEQ = 4096
DEPTH = 1

EXPAND = 2
D_MIX = EXPAND * D_MODEL
D_CONV = D_MIX // 2
D_LRU = D_MIX - D_CONV
N_CONV_HEADS = 8
N_LRU_HEADS = 16
LRU_HEAD_DIM = D_LRU // N_LRU_HEADS
SHORT_CONV_WIDTH = 3
LRU_CONV_WIDTH = 4
RG_LRU_C = 8.0
RMS_EPS = 1e-6
IN_COLS = 4 * D_CONV + 2 * D_LRU

kernel_name = "hymba_shortconv_rglru_hybrid"


def rms_norm(x, g):
    xf = x.astype(jnp.float32)
    xf = xf * lax.rsqrt(jnp.mean(xf * xf, axis=-1, keepdims=True) + RMS_EPS)
    return xf.astype(x.dtype) * g


def headwise_rms_norm(y, n_heads, g):
    b, s, d = y.shape
    yh = y.reshape(b, s, n_heads, d // n_heads).astype(jnp.float32)
    yh = yh * lax.rsqrt(jnp.mean(yh * yh, axis=-1, keepdims=True) + RMS_EPS)
    return yh.reshape(b, s, d).astype(y.dtype) * g


def causal_depthwise_conv(u, w):
    k_width = w.shape[0]
    s = u.shape[1]
    up = jnp.pad(u, ((0, 0), (k_width - 1, 0), (0, 0)))
    out = up[:, 0:s, :] * w[0]
    for k in range(1, k_width):
        out = out + up[:, k:k + s, :] * w[k]
    return out


def short_conv_mixer(b_gate, c_gate, x_in, conv_w):
    return b_gate * causal_depthwise_conv(c_gate * x_in, conv_w)


def _lru_combine(left, right):
    a1, b1 = left
    a2, b2 = right
    return a1 * a2, a2 * b1 + b2


def rg_lru_mixer(x_in, conv_w, conv_b, w_a, b_a, w_i, b_i, lam):
    bsz, s, d = x_in.shape
    u = causal_depthwise_conv(x_in, conv_w) + conv_b
    uh = u.reshape(bsz, s, N_LRU_HEADS, LRU_HEAD_DIM)
    r = jax.nn.sigmoid(jnp.einsum('bshd,hde->bshe', uh, w_a).reshape(bsz, s, d) + b_a)
    i = jax.nn.sigmoid(jnp.einsum('bshd,hde->bshe', uh, w_i).reshape(bsz, s, d) + b_i)
    log_a = RG_LRU_C * r.astype(jnp.float32) * jax.nn.log_sigmoid(lam.astype(jnp.float32))
    a = jnp.exp(log_a)
    mult = jnp.sqrt(-jnp.expm1(2.0 * log_a))
    drive = mult * (i * u).astype(jnp.float32)
    _, h = lax.associative_scan(_lru_combine, (a, drive), axis=1)
    return h.astype(x_in.dtype)


def setup_inputs(seed: int = 0) -> dict:
    key = jax.random.key(seed)
    ks = jax.random.split(key, 16)
    f32 = jnp.float32
    x = jax.random.normal(ks[0], (BATCH, SEQ, D_MODEL), f32)
    ln_g = 1.0 + 0.02 * jax.random.normal(ks[1], (D_MODEL,), f32)
    w_in = jax.random.normal(ks[2], (D_MODEL, IN_COLS), f32) * D_MODEL ** -0.5
    conv_w = jax.random.normal(ks[3], (SHORT_CONV_WIDTH, D_CONV), f32) * SHORT_CONV_WIDTH ** -0.5
    lru_conv_w = jax.random.normal(ks[4], (LRU_CONV_WIDTH, D_LRU), f32) * LRU_CONV_WIDTH ** -0.5
    lru_conv_b = 0.02 * jax.random.normal(ks[5], (D_LRU,), f32)
    w_a = jax.random.normal(ks[6], (N_LRU_HEADS, LRU_HEAD_DIM, LRU_HEAD_DIM), f32) * LRU_HEAD_DIM ** -0.5
    b_a = 0.02 * jax.random.normal(ks[7], (D_LRU,), f32)
    w_i = jax.random.normal(ks[8], (N_LRU_HEADS, LRU_HEAD_DIM, LRU_HEAD_DIM), f32) * LRU_HEAD_DIM ** -0.5
    b_i = 0.02 * jax.random.normal(ks[9], (D_LRU,), f32)
    a_init = jax.random.uniform(ks[10], (D_LRU,), f32, minval=0.9, maxval=0.999)
    lam = jnp.log(a_init) - jnp.log1p(-a_init)
    conv_out_g = 1.0 + 0.02 * jax.random.normal(ks[11], (D_CONV,), f32)
    lru_out_g = 1.0 + 0.02 * jax.random.normal(ks[12], (D_LRU,), f32)
    w_out = jax.random.normal(ks[13], (D_MIX, D_MODEL), f32) * D_MIX ** -0.5
    final_g = 1.0 + 0.02 * jax.random.normal(ks[14], (D_MODEL,), f32)
    return {"x": x, "ln_g": ln_g, "w_in": w_in, "conv_w": conv_w,
            "lru_conv_w": lru_conv_w, "lru_conv_b": lru_conv_b,
            "w_a": w_a, "b_a": b_a, "w_i": w_i, "b_i": b_i, "lam": lam,
            "conv_out_g": conv_out_g, "lru_out_g": lru_out_g,
            "w_out": w_out, "final_g": final_g}


def reference(x, ln_g, w_in, conv_w, lru_conv_w, lru_conv_b, w_a, b_a, w_i, b_i,
              lam, conv_out_g, lru_out_g, w_out, final_g):
    h = x
    for _ in range(DEPTH):
        xn = rms_norm(h, ln_g)
        proj = jnp.einsum('bsd,de->bse', xn, w_in)
        splits = [D_CONV, 2 * D_CONV, 3 * D_CONV, 4 * D_CONV, 4 * D_CONV + D_LRU]
        b_gate, c_gate, x_conv, g_conv, x_lru, g_lru = jnp.split(proj, splits, axis=-1)
        y_conv = short_conv_mixer(b_gate, c_gate, x_conv, conv_w)
        y_conv = headwise_rms_norm(y_conv, N_CONV_HEADS, conv_out_g) * jax.nn.silu(g_conv)
        y_lru = rg_lru_mixer(x_lru, lru_conv_w, lru_conv_b, w_a, b_a, w_i, b_i, lam)
        y_lru = headwise_rms_norm(y_lru, N_LRU_HEADS, lru_out_g) * jax.nn.silu(g_lru)
        y = jnp.concatenate([y_conv, y_lru], axis=-1)
        h = h + jnp.einsum('bse,ed->bsd', y, w_out)
    return rms_norm(h, final_g)
```

```python
import math
from contextlib import ExitStack

import numpy as np
import ml_dtypes
import concourse.bass as bass
import concourse.mybir as mybir
from concourse.bass_utils import run_bass_kernel_spmd

F32 = mybir.dt.float32
BF16 = mybir.dt.bfloat16
AF = mybir.ActivationFunctionType
ALU = mybir.AluOpType

ENGS = ("pe", "act", "dve", "pool", "sp")
TBL_PEN = 1.0
JITTER_SEED = 6
EVAC_PRIO = 1
EVAC_RES = set()


class Op:
    __slots__ = ("idx", "eng", "emit", "reads", "writes", "dur", "deps", "kind",
                 "dma_key", "lat", "name", "signals", "count", "fin", "start", "tbl", "prio")

    def __init__(self):
        self.deps = set()
        self.signals = False
        self.count = 0


class Prog:
    def __init__(self):
        self.ops = []
        self.last_w = {}
        self.readers = {}
        self.final_dma = []

    def add(self, eng, emit, reads=(), writes=(), dur=0.3, kind="c", dma_key=None,
            lat=0.0, name="", final=False, tbl=None):
        op = Op()
        op.tbl = tbl
        op.prio = 1
        op.idx = len(self.ops)
        op.eng, op.emit, op.dur, op.kind = eng, emit, dur, kind
        op.dma_key, op.lat, op.name = dma_key, lat, name
        op.reads, op.writes = tuple(reads), tuple(writes)
        if eng != "pe" and kind != "dma" and any(r in EVAC_RES for r in op.reads):
            op.prio = 0
        if kind == "dma":
            op.prio = 0
        for r in op.reads:
            w = self.last_w.get(r)
            if w is not None:
                op.deps.add(w)
        for r in op.writes:
            w = self.last_w.get(r)
            if w is not None:
                op.deps.add(w)
            for rd in self.readers.get(r, ()):
                op.deps.add(rd)
        for r in op.reads:
            self.readers.setdefault(r, []).append(op.idx)
        for r in op.writes:
            self.last_w[r] = op.idx
            self.readers[r] = []
        op.deps.discard(op.idx)
        self.ops.append(op)
        if final:
            self.final_dma.append(op.idx)
        return op

    def schedule(self, sync_lat=0.30):
        ops = self.ops
        n = len(ops)
        if JITTER_SEED:
            import random
            rng = random.Random(JITTER_SEED)
            for o in ops:
                o.dur *= 1.0 + 0.08 * (rng.random() - 0.5)
        succ = [[] for _ in range(n)]
        npred = [0] * n
        for o in ops:
            npred[o.idx] = len(o.deps)
            for d in o.deps:
                succ[d].append(o.idx)
        eng_free = {e: 0.0 for e in ENGS}
        avail = {e: {} for e in ENGS}
        dma_free = {e: 0.0 for e in ENGS}
        for o in ops:
            o.fin = None
            if npred[o.idx] == 0:
                avail[o.eng][o.idx] = 0.0
        order = {e: [] for e in ENGS}
        done = 0
        cur_tbl = None
        self.nswitch = 0
        while done < n:
            best = None
            for e in ENGS:
                a = avail[e]
                if not a:
                    continue
                ef = eng_free[e]
                if e == "act":
                    c = min((max(rt, ef) + (TBL_PEN if (ops[i].tbl is not None and ops[i].tbl != cur_tbl) else 0.0),
                             ops[i].prio * EVAC_PRIO, i) for i, rt in a.items())
                else:
                    c = min((max(rt, ef), ops[i].prio * EVAC_PRIO, i) for i, rt in a.items())
                if best is None or c < best[0]:
                    best = (c, e)
            (st, _pr, idx), e = best
            del avail[e][idx]
            o = ops[idx]
            if e == "act" and o.tbl is not None and o.tbl != cur_tbl:
                st = st - TBL_PEN + 1.3
                cur_tbl = o.tbl
                self.nswitch += 1
            o.start = st
            eng_free[e] = st + o.dur
            if o.kind == "dma":
                t0 = max(st + o.dur, dma_free[e])
                dma_free[e] = t0 + o.lat
                o.fin = t0 + o.lat + 2.0
            else:
                o.fin = st + o.dur
            order[e].append(idx)
            done += 1
            for s in succ[idx]:
                npred[s] -= 1
                if npred[s] == 0:
                    so = ops[s]
                    rt = 0.0
                    for d in so.deps:
                        rt = max(rt, ops[d].fin + sync_lat)
                    avail[so.eng][s] = rt
        self.order = order
        self.makespan = max(o.fin for o in ops)
        self.busy = {e: sum(ops[i].dur for i in order[e]) for e in ENGS}
        return order

    def emit(self, handles, sems, get_dma_sem):
        ops = self.ops
        order = self.order
        for o in ops:
            for d in o.deps:
                p = ops[d]
                if p.eng == "pe" and o.eng == "pe" and p.kind != "dma" and o.kind != "dma":
                    continue
                p.signals = True
        for i in self.final_dma:
            ops[i].signals = True
        cnt = {e: 0 for e in ENGS}
        dcnt = {}
        for e in ENGS:
            for idx in order[e]:
                o = ops[idx]
                if o.kind == "dma":
                    dcnt[o.dma_key] = dcnt.get(o.dma_key, 0) + 1
                    o.count = dcnt[o.dma_key] * 16
                    o.signals = True
                elif o.signals:
                    cnt[e] += 1
                    o.count = cnt[e]
        nwait = 0
        for e in ENGS:
            h = handles[e]
            waited = {}
            for idx in order[e]:
                o = ops[idx]
                need = {}
                for d in o.deps:
                    p = ops[d]
                    if p.kind == "dma":
                        key = ("d", p.dma_key)
                        sem = get_dma_sem(p.dma_key)
                    else:
                        if p.eng == "pe" and e == "pe" and o.kind != "dma":
                            continue
                        key = ("e", p.eng)
                        sem = sems[p.eng]
                    if need.get(key, (None, 0))[1] < p.count:
                        need[key] = (sem, p.count)
                for key, (sem, val) in need.items():
                    if waited.get(key, 0) < val:
                        h.wait_ge(sem, val)
                        waited[key] = val
                        nwait += 1
                inst = o.emit(h)
                if o.signals:
                    if o.kind == "dma":
                        inst.then_inc(get_dma_sem(o.dma_key), 16)
                    else:
                        inst.then_inc(sems[e], 1)
        h = handles["sp"]
        fin = {}
        for i in self.final_dma:
            o = ops[i]
            fin[o.dma_key] = max(fin.get(o.dma_key, 0), o.count)
        for k, v in fin.items():
            h.wait_ge(get_dma_sem(k), v)
        self.nwait = nwait


D = 1024
NK = 8
TC = 512
EPS = 1e-6
NPAR = 14
NCS = 2
NLS = 3
NWS = 3
NXIN = 3
SPLIT = (8, 8)
WOUT_N = 1
PJB = (0, 1, 2, 3, 6)
SQ_POOL = False
LATE_G = True
IN_J = (0, 1, 2, 3)
OUT_J = (3, 4, 5, 6)
TAIL_OPT = True
TAIL_BOUNDS = (0, 4, 6, 7, 8)
TAIL_LAG = 2
OPB_TAIL = (7, 6, 0, 1, 2, 3)
BANKS = {"gr": 4, "gi": 4, "cs": 4, "ls": 5}
OPB = (7,)
LN_HALF = math.log(0.5)

T_MM = 0.222
T_ACT = 0.6
T_ACT_P = 0.56
T_DVE_TT = 0.79
T_DVE_TS = 0.58
T_DVE_PS = 0.65
T_SCAN = 1.25
T_POOL_TT = 1.4
T_POOL_CP = 0.75
T_TINY = 0.15
T_PTINY = 0.44


def build_program(NT=8, taps=False):
    S = NT * TC
    NTT = S // 128
    nc = bass.Bass("TRN2", target_bir_lowering=False)
    dr = lambda name, shape, dt, kind="ExternalInput": nc.dram_tensor(name, shape, dt, kind=kind).ap()
    x_d = dr("x", [S, D], F32)
    win_d = dr("w_in_r", [8, 128, NK * 768], F32)
    wout_d = dr("w_out_r", [128, 16 * D], F32)
    wa_d = dr("w_a", [16, 64, 64], F32)
    wi_d = dr("w_i", [16, 64, 64], F32)
    par_d = dr("par", [NPAR, D], F32)
    lng_d = dr("ln_g", [1, D], F32)
    fg_d = dr("final_g", [1, D], F32)
    idb_d = dr("ident_bf", [128, 128], BF16)
    idf_d = dr("ident_f", [128, 128], F32)
    ones_d = dr("ones_bf", [128, 128], BF16)
    blk_d = dr("blk_bf", [128, 128], BF16)
    out_d = dr("out", [S, D], F32, kind="ExternalOutput")
    wsc_d = dr("wsc", [8, 128, NK * 768], BF16, kind="Internal")

    es = ExitStack()
    sb = lambda name, shape, dt: es.enter_context(nc.sbuf_tensor(name, shape, dt))
    wbuf = [sb("wbuf%d" % i, [128, NK, 768], BF16) for i in range(NWS)]
    wout = sb("wout", [128, 16, D], BF16)
    gw = sb("gw", [128, 2, 8, 128], BF16)
    idb = sb("idb", [128, 128], BF16)
    idf = sb("idf", [128, 128], F32)
    ones = sb("ones", [128, 128], BF16)
    blk = sb("blk", [128, 128], BF16)
    lng = sb("lng", [128, D], F32)
    fgb = sb("fgb", [128, D], F32)
    PAR = sb("PAR", [128, 8, 16], F32)
    xin = [sb("xin%d" % i, [128, D], F32) for i in range(NXIN)]
    xr = [sb("xr%d" % i, [128, D], F32) for i in range(2)]
    xn = [sb("xn%d" % i, [128, D], BF16) for i in range(2)]
    junk = sb("junk", [128, D], BF16)
    st_i = [sb("sti%d" % i, [128, 4], F32) for i in range(2)]
    st_o = [sb("sto%d" % i, [128, 4], F32) for i in range(2)]
    mhalf = sb("mhalf", [128, 1], F32)
    xnT = [sb("xnT%d" % i, [128, NK, TC], BF16) for i in range(2)]
    yT = [sb("yT%d" % i, [128, 16, TC], BF16) for i in range(2)]
    cA = [sb("cA%d" % i, [128, TC + 2], F32) for i in range(NCS)]
    cB = [sb("cB%d" % i, [128, TC], F32) for i in range(NCS)]
    cSG = [sb("cSG%d" % i, [128, TC], F32) for i in range(NCS)]
    cY2 = [sb("cY2%d" % i, [128, TC], BF16) for i in range(NCS)]
    l1 = [sb("l1_%d" % i, [128, TC + 3], F32) for i in range(NLS)]
    l2 = [sb("l2_%d" % i, [128, TC], F32) for i in range(NLS)]
    l3 = [sb("l3_%d" % i, [128, TC], F32) for i in range(NLS)]
    l4 = [sb("l4_%d" % i, [128, TC], F32) for i in range(NLS)]
    l5 = [sb("l5_%d" % i, [128, TC], F32) for i in range(NLS)]
    lUB = [sb("lUB%d" % i, [128, TC], BF16) for i in range(NLS)]
    lH2 = [sb("lH2%d" % i, [128, TC], BF16) for i in range(NLS)]
    HC = sb("HC", [128, 8, 2], F32)
    HL = sb("HL", [128, 8, 4], F32)
    HS = sb("HS", [128, 8], F32)
    ps = es.enter_context(nc.psum_tensor("ps", [128, 7, TC], F32))
    psT = es.enter_context(nc.psum_tensor("psT", [128, 2 * TC], BF16))
    bank = [ps[:, b, :] for b in range(7)] + [psT[:].bitcast(F32)]

    sems = {e: es.enter_context(nc.semaphore("s_" + e)) for e in ("pe", "act", "dve", "pool")}
    dsems = {}

    def get_dma_sem(k):
        if k not in dsems:
            dsems[k] = es.enter_context(nc.semaphore("d_" + k))
        return dsems[k]

    P = Prog()
    tapn = [0]
    psres = lambda b: "psT" if b == 7 else "ps%d" % b

    def tap(name, ap, res, shape, dt=F32, cond=True):
        if not (taps and cond):
            return
        t = nc.dram_tensor("tap_" + name, list(shape), dt, kind="ExternalOutput").ap()
        tapn[0] += 1
        P.add("sp", (lambda h: h.dma_start(out=t, in_=ap)), reads=list(res), writes=[], kind="dma",
              dma_key="tap%d" % tapn[0], lat=1.0, dur=0.1, final=True)

    def dma(q, out, in_, reads, writes, key, nbytes, final=False, name="", **kw):
        return P.add(q, (lambda h: h.dma_start(out=out, in_=in_, **kw)), reads=reads, writes=writes,
                     kind="dma", dma_key=key, lat=nbytes / 220e3, dur=(1.5 if q == "pool" else 0.1),
                     final=final, name=name)

    epsb = sb("epsb", [128, 1], F32)
    oneb = sb("oneb", [128, 1], F32)
    lnhb = sb("lnhb", [128, 1], F32)
    P.add("pool", lambda h: h.memset(epsb[:], EPS), writes=["epsb"], dur=T_TINY)
    P.add("pool", lambda h: h.memset(oneb[:], 1.0), writes=["oneb"], dur=T_TINY)
    P.add("pool", lambda h: h.memset(lnhb[:], LN_HALF), writes=["lnhb"], dur=T_TINY)

    bc = lambda t: bass.AP(t.tensor, 0, [[0, 128], [1, D]])
    dma("sp", idb[:], idb_d, [], ["idb"], "c0", 32768)
    dma("sp", idf[:], idf_d, [], ["idf"], "c1", 65536)
    dma("sp", xr[0][0:NPAR, :], par_d, [], ["xr0"], "c4", NPAR * 4096)
    dma("sp", lng[:], bc(lng_d), [], ["lng"], "c5", 524288)
    for g0, (XB0, rX0, kX0) in enumerate([(xin[0], "xin0", "xin0"), (xin[1], "xin1", "xin1"), (xin[2], "xin2", "xin2"),
                                          (xr[1], "xr1", "xrl1")]):
        dma("sp", XB0[:], x_d[g0 * 128:(g0 + 1) * 128, :], [], [rX0], kX0, 524288)

    dma("sp", ones[:], ones_d, [], ["ones"], "c2", 32768)
    dma("sp", blk[:], blk_d, [], ["blk"], "c3", 32768)
    dma("sp", fgb[:], bc(fg_d), [], ["fgb"], "c6", 524288)
    P.add("pool", lambda h: h.memset(mhalf[:], -0.5), writes=["mhalf"], dur=T_TINY)
    P.add("pool", lambda h: h.memset(HC[:], 0.0), writes=["HC%d" % j for j in range(8)], dur=T_TINY)
    P.add("pool", lambda h: h.memset(HL[:], 0.0), writes=["HL%d" % j for j in range(8)], dur=T_TINY)
    P.add("pool", lambda h: h.memset(HS[:], 0.0), writes=["HS%d" % j for j in range(8)], dur=T_TINY)
    P.add("pool", lambda h: h.memset(gw[:], 0.0), writes=["gw"], dur=1.0)

    def emit_par_T(h):
        last = None
        for j in range(8):
            last = h.transpose(out=bank[4][:, j * 16:j * 16 + NPAR], in_=xr[0][0:NPAR, j * 128:(j + 1) * 128],
                               identity=idf[0:NPAR, 0:NPAR])
        return last
    P.add("pe", emit_par_T, reads=["xr0", "idf"], writes=["ps4"], dur=1.0)
    P.add("dve", lambda h: h.tensor_copy(out=PAR[:, :, 0:NPAR],
                                         in_=bank[4][:, 0:128].rearrange("p (j c) -> p j c", c=16)[:, :, 0:NPAR]),
          reads=["ps4"], writes=["PAR"], dur=0.3)
    P.add("act", lambda h: h.activation(out=PAR[:, :, 13], in_=PAR[:, :, 10], func=AF.Exp, scale=-1.0), tbl="E",
          reads=["PAR"], writes=["PARd"], dur=0.3)
    P.add("act", lambda h: h.activation(out=PAR[:, :, 13], in_=PAR[:, :, 13], func=AF.Ln, bias=oneb[:]), tbl="E",
          reads=["PARd", "oneb"], writes=["PARd"], dur=0.3)
    P.add("dve", lambda h: h.tensor_scalar(out=PAR[:, :, 13], in0=PAR[:, :, 13], scalar1=-4.0, scalar2=None,
                                           op0=ALU.mult), reads=["PARd"], writes=["PARd"], dur=0.2)
    P.add("dve", lambda h: h.tensor_scalar(out=PAR[:, :, 14:16], in0=PAR[:, :, 8:10], scalar1=0.5, scalar2=None,
                                           op0=ALU.mult), reads=["PAR"], writes=["PARe"], dur=0.2)
    PARR = ["PAR", "PARd", "PARe"]
    tap("PAR", PAR[:], PARR, [128, 8, 16])

    for gi, wd in enumerate((wa_d, wi_d)):
        src = wd.rearrange("(c two) d e -> two d c e", two=2)
        for two in range(2):
            dma("pool", gw[two * 64:(two + 1) * 64, gi, :, two * 64:(two + 1) * 64], src[two],
                ["gw"], ["gw%d%d" % (gi, two)], "gw%d%d" % (gi, two), 131072)
    GWR = ["gw%d%d" % (a, b) for a in range(2) for b in range(2)]

    wcount = [0]
    late_ok = [False]

    def load_group(n, j):
        slot = wcount[0] % NWS
        wcount[0] += 1
        parts = ["wb%d.%d" % (slot, s) for s in range(6)]
        ready_n = 1 if j < SPLIT[0] else (2 if j < SPLIT[1] else 3)
        cast_load = n < ready_n
        if cast_load:
            hold = ["xr1"] if (n == 0 and j in (1, 2)) else (["xin1"] if (n == 0 and j == 0) else [])
            dma("pool", wbuf[slot][:].rearrange("p k e -> p (k e)").rearrange("p (a b) -> p a b", b=2048),
                win_d[j].rearrange("p (a b) -> p a b", b=2048), hold, parts, "wb%d" % slot, 128 * 6144 * 4)
            if NT > n + 1 and n == ready_n - 1:
                dma("sp", wsc_d[j], wbuf[slot][:].rearrange("p k e -> p (k e)"), parts, ["wsc%d" % j],
                    "ws%d" % slot, 1572864)
        else:
            dma("sp", wbuf[slot][:].rearrange("p k e -> p (k e)"), wsc_d[j], ["wsc%d" % j], parts,
                "wl%d" % slot, 1572864)
        late_ok[0] = not cast_load
        return slot, parts

    def load_wout(q):
        src = wout_d.rearrange("p (c d) -> p c d", d=D)
        for hh in range(2):
            dma("pool", wout[:, q * 4 + 2 * hh:q * 4 + 2 * hh + 2, :], src[:, q * 4 + 2 * hh:q * 4 + 2 * hh + 2, :],
                ["wsc7"], ["wout%d.%d" % (q, hh)], "wo%d%d" % (q, hh), 1048576)
    WOUTR = ["wout%d.%d" % (q, hh) for q in range(4) for hh in range(2)]

    def in_tile(g):
        n, tt = divmod(g, 4)
        ns = n % 2
        if g == 3:
            XB, rX, kX = xr[1], "xr1", "xrl1"
        else:
            s = (g if g < 3 else g - 1) % NXIN
            XB, rX, kX = xin[s], "xin%d" % s, "xin%d" % s
        if g >= 4:
            dma("sp", XB[:], x_d[g * 128:(g + 1) * 128, :], [], [rX], kX, 524288)
        s2 = g % 2
        P.add("act", lambda h: h.activation(out=junk[:], in_=XB[:], func=AF.Square, accum_out=st_i[s2][:, 0:1]),
              reads=[rX], writes=["junk", "sti%d" % s2], dur=1.15)
        P.add("dve", lambda h: h.tensor_scalar(out=st_i[s2][:, 1:2], in0=st_i[s2][:, 0:1], scalar1=1.0 / D,
                                               scalar2=EPS, op0=ALU.mult, op1=ALU.add),
              reads=["sti%d" % s2], writes=["sti%db" % s2], dur=T_TINY)
        P.add("pool", lambda h: h.tensor_tensor(out=st_i[s2][:, 2:3], in0=st_i[s2][:, 1:2], in1=mhalf[:], op=ALU.pow),
              reads=["sti%db" % s2, "mhalf"], writes=["sti%dc" % s2], dur=0.5)
        P.add("dve", lambda h: h.scalar_tensor_tensor(out=xn[s2][:], in0=XB[:], scalar=st_i[s2][:, 2:3], in1=lng[:],
                                                      op0=ALU.mult, op1=ALU.mult),
              reads=[rX, "sti%dc" % s2, "lng"], writes=["xn%d" % s2], dur=1.25)
        def emit_T(h):
            last = None
            for k in range(NK):
                last = h.transpose(out=psT[:, k * 128:(k + 1) * 128], in_=xn[s2][:, k * 128:(k + 1) * 128],
                                   identity=idb[:])
            return last
        P.add("pe", emit_T, reads=["xn%d" % s2, "idb"], writes=["psT"], dur=0.9)
        P.add("dve", lambda h: h.tensor_copy(out=xnT[ns][:, :, tt * 128:(tt + 1) * 128],
                                             in_=psT[:].rearrange("p (k t) -> p k t", t=128)),
              reads=["psT"], writes=["xnT%d.%d" % (ns, tt)], dur=0.75)

    axc = [0]
    tail_done = []
    pjc = [0]
    ccnt = [0]
    lcnt = [0]

    def pj_bank():
        b = PJB[pjc[0] % len(PJB)]
        pjc[0] += 1
        EVAC_RES.add("ps%d" % b)
        return b

    def ax_bank(kind):
        return BANKS[kind]

    def proj(n, wslot, wres, col0, b):
        ns = n % 2

        def emit(h):
            last = None
            for k in range(NK):
                last = h.matmul(bank[b], lhsT=wbuf[wslot][:, k, col0:col0 + 128], rhs=xnT[ns][:, k, :],
                                start=(k == 0), stop=(k == NK - 1))
            return last
        P.add("pe", emit, reads=[wres] + ["xnT%d.%d" % (ns, t) for t in range(4)], writes=["ps%d" % b],
              dur=NK * T_MM)

    def conv_chunk(n, j, wslot, parts):
        sc = ccnt[0] % NCS
        ccnt[0] += 1
        ys = n % 2
        A, Bt, SG, Y2 = cA[sc], cB[sc], cSG[sc], cY2[sc]
        rA, rAh, rB, rSG, rY2 = "cA%d" % sc, "cAh%d" % sc, "cB%d" % sc, "cSG%d" % sc, "cY2%d" % sc
        pw = lambda c: PAR[:, j, c:c + 1]
        b_xc = pj_bank()
        proj(n, wslot, parts[0], 0, b_xc)
        P.add("act", lambda h: h.activation(out=Bt[:], in_=bank[b_xc], func=AF.Copy),
              reads=["ps%d" % b_xc], writes=[rB], dur=T_ACT_P)
        T0 = (n == 0 and j == 0)
        tap("c_xc", Bt[:], [rB], [128, TC], cond=T0)
        P.add("pool", lambda h: h.tensor_copy(out=A[:, 0:2], in_=HC[:, j, :]), reads=["HC%d" % j], writes=[rAh],
              dur=T_PTINY)
        b_c = pj_bank()
        proj(n, wslot, parts[1], 128, b_c)
        P.add("dve", lambda h: h.tensor_tensor(out=A[:, 2:TC + 2], in0=bank[b_c], in1=Bt[:], op=ALU.mult),
              reads=["ps%d" % b_c, rB], writes=[rA], dur=T_DVE_PS)
        P.add("pool", lambda h: h.tensor_copy(out=HC[:, j, :], in_=A[:, TC:TC + 2]), reads=[rA], writes=["HC%d" % j],
              dur=T_PTINY)
        tap("c_cx", A[:], [rA, rAh], [128, TC + 2], cond=T0)
        def do_g():
            b_g = pj_bank()
            proj(n, wslot, parts[2], 256, b_g)
            P.add("act", lambda h: h.activation(out=SG[:], in_=bank[b_g], func=AF.Silu), tbl="B",
                  reads=["ps%d" % b_g], writes=[rSG], dur=T_ACT_P)
            tap("c_sg", SG[:], [rSG], [128, TC], cond=T0)
        LG = LATE_G and late_ok[0]
        if not LG:
            do_g()
        P.add("dve", lambda h: h.tensor_scalar(out=Bt[:], in0=A[:, 2:TC + 2], scalar1=pw(2), scalar2=None, op0=ALU.mult),
              reads=[rA] + PARR, writes=[rB], dur=T_DVE_TS)
        P.add("dve", lambda h: h.scalar_tensor_tensor(out=Bt[:], in0=A[:, 1:TC + 1], scalar=pw(1), in1=Bt[:],
                                                      op0=ALU.mult, op1=ALU.add),
              reads=[rA, rAh, rB] + PARR, writes=[rB], dur=T_DVE_TT)
        P.add("dve", lambda h: h.scalar_tensor_tensor(out=Bt[:], in0=A[:, 0:TC], scalar=pw(0), in1=Bt[:],
                                                      op0=ALU.mult, op1=ALU.add),
              reads=[rA, rAh, rB] + PARR, writes=[rB], dur=T_DVE_TT)
        tap("c_cv", Bt[:], [rB], [128, TC], cond=T0)
        b_b = pj_bank()
        proj(n, wslot, parts[3], 384, b_b)
        P.add("dve", lambda h: h.tensor_tensor(out=Bt[:], in0=bank[b_b], in1=Bt[:], op=ALU.mult),
              reads=["ps%d" % b_b, rB], writes=[rB], dur=T_DVE_PS)
        tap("c_y", Bt[:], [rB], [128, TC], cond=T0)
        if LG:
            do_g()
        if SQ_POOL:
            P.add("pool", lambda h: h.tensor_tensor(out=Y2[:], in0=Bt[:], in1=Bt[:], op=ALU.mult), reads=[rB], writes=[rY2],
                  dur=T_POOL_TT)
        else:
            P.add("act", lambda h: h.activation(out=Y2[:], in_=Bt[:], func=AF.Square), reads=[rB], writes=[rY2], dur=T_ACT)
        ax = ax_bank("cs")
        P.add("pe", lambda h: h.matmul(bank[ax], lhsT=ones[:], rhs=Y2[:], start=True, stop=True),
              reads=[rY2, "ones"], writes=[psres(ax)], dur=T_MM)
        P.add("act", lambda h: h.activation(out=A[:, 0:TC], in_=bank[ax], func=AF.Ln, scale=1.0 / 128, bias=epsb[:]), tbl="E",
              reads=[psres(ax), "epsb"], writes=[rA, rAh], dur=T_ACT_P)
        P.add("act", lambda h: h.activation(out=A[:, 0:TC], in_=A[:, 0:TC], func=AF.Exp, scale=-0.5), tbl="E",
              reads=[rA, rAh], writes=[rA, rAh], dur=T_ACT)
        tap("c_rstd", A[:, 0:TC], [rA, rAh], [128, TC], cond=T0)
        P.add("dve", lambda h: h.scalar_tensor_tensor(out=Bt[:], in0=Bt[:], scalar=pw(11), in1=A[:, 0:TC],
                                                      op0=ALU.mult, op1=ALU.mult),
              reads=[rB, rA, rAh] + PARR, writes=[rB], dur=T_DVE_TT)
        P.add("pool", lambda h: h.tensor_tensor(out=yT[ys][:, j, :], in0=Bt[:], in1=SG[:], op=ALU.mult),
              reads=[rB, rSG], writes=["yT%d.%d" % (ys, j)], dur=T_POOL_TT)
        tap("c_yT", yT[ys][:, j, :], ["yT%d.%d" % (ys, j)], [128, TC], BF16, cond=T0)

    def lru_chunk(n, j, wslot, parts):
        sl = lcnt[0] % NLS
        lcnt[0] += 1
        ys = n % 2
        T1, T2, T3, T4, T5, UB, H2 = l1[sl], l2[sl], l3[sl], l4[sl], l5[sl], lUB[sl], lH2[sl]
        r1, r1h, r2, r3, r4, r5, rUB, rH2 = ["l%s_%d" % (t, sl) for t in ("1", "1h", "2", "3", "4", "5", "UB", "H2")]
        pw = lambda c: PAR[:, j, c:c + 1]
        b_x = pj_bank()
        proj(n, wslot, parts[4], 512, b_x)
        P.add("act", lambda h: h.activation(out=T1[:, 3:TC + 3], in_=bank[b_x], func=AF.Copy),
              reads=["ps%d" % b_x], writes=[r1], dur=T_ACT_P)
        P.add("pool", lambda h: h.tensor_copy(out=T1[:, 0:3], in_=HL[:, j, 0:3]), reads=["HL%d" % j], writes=[r1h],
              dur=T_PTINY)
        P.add("pool", lambda h: h.tensor_copy(out=HL[:, j, 0:3], in_=T1[:, TC:TC + 3]), reads=[r1], writes=["HL%d" % j],
              dur=T_PTINY)
        def do_gl():
            b_g = pj_bank()
            proj(n, wslot, parts[5], 640, b_g)
            P.add("act", lambda h: h.activation(out=T5[:], in_=bank[b_g], func=AF.Silu), tbl="B",
                  reads=["ps%d" % b_g], writes=[r5], dur=T_ACT_P)
        LG = LATE_G and late_ok[0]
        if not LG:
            do_gl()
        P.add("dve", lambda h: h.tensor_scalar(out=T2[:], in0=T1[:, 3:TC + 3], scalar1=pw(6), scalar2=pw(7),
                                               op0=ALU.mult, op1=ALU.add),
              reads=[r1] + PARR, writes=[r2], dur=T_DVE_TS)
        for k, eng in ((1, "dve"), (2, "dve"), (3, "dve")):
            P.add(eng, (lambda h, k=k: h.scalar_tensor_tensor(out=T2[:], in0=T1[:, 3 - k:TC + 3 - k], scalar=pw(6 - k),
                                                              in1=T2[:], op0=ALU.mult, op1=ALU.add)),
                  reads=[r1, r1h, r2] + PARR, writes=[r2], dur=(T_DVE_TT if eng == "dve" else T_POOL_TT))
        T0 = (n == 0 and j == 0)
        tap("l_xl", T1[:], [r1, r1h], [128, TC + 3], cond=T0)
        tap("l_u", T2[:], [r2], [128, TC], cond=T0)
        P.add("dve", lambda h: h.tensor_copy(out=UB[:], in_=T2[:]), reads=[r2], writes=[rUB], dur=T_DVE_TS)
        ax_r = ax_bank("gr")
        P.add("pe", lambda h: h.matmul(bank[ax_r], lhsT=gw[:, 0, j, :], rhs=UB[:], start=True, stop=True),
              reads=[rUB] + GWR, writes=[psres(ax_r)], dur=T_MM)
        P.add("act", lambda h: h.activation(out=T3[:], in_=bank[ax_r], func=AF.Tanh, scale=0.5, bias=pw(14)), tbl="B",
              reads=[psres(ax_r)] + PARR, writes=[r3], dur=T_ACT_P + 0.1)
        ax_i = ax_bank("gi")
        P.add("pe", lambda h: h.matmul(bank[ax_i], lhsT=gw[:, 1, j, :], rhs=UB[:], start=True, stop=True),
              reads=[rUB] + GWR, writes=[psres(ax_i)], dur=T_MM)
        P.add("act", lambda h: h.activation(out=T4[:], in_=bank[ax_i], func=AF.Tanh, scale=0.5, bias=pw(15)), tbl="B",
              reads=[psres(ax_i)] + PARR, writes=[r4], dur=T_ACT_P + 0.1)
        if LG:
            do_gl()
        tap("l_tr", T3[:], [r3], [128, TC], cond=T0)
        tap("l_ti", T4[:], [r4], [128, TC], cond=T0)
        P.add("act", lambda h: h.activation(out=T3[:], in_=T3[:], func=AF.Exp, scale=pw(13), bias=pw(13)), tbl="E",
              reads=[r3] + PARR, writes=[r3], dur=T_ACT + 0.2)
        P.add("pool", lambda h: h.tensor_tensor(out=T1[:, 0:TC], in0=T3[:], in1=T3[:], op=ALU.mult),
              reads=[r3], writes=[r1, r1h], dur=T_POOL_TT)
        P.add("act", lambda h: h.activation(out=T1[:, 0:TC], in_=T1[:, 0:TC], func=AF.Ln, scale=-1.0, bias=oneb[:]), tbl="E",
              reads=[r1, r1h, "oneb"], writes=[r1, r1h], dur=T_ACT)
        P.add("act", lambda h: h.activation(out=T1[:, 0:TC], in_=T1[:, 0:TC], func=AF.Exp, scale=0.5, bias=lnhb[:]), tbl="E",
              reads=[r1, r1h, "lnhb"], writes=[r1, r1h], dur=T_ACT)
        tap("l_a", T3[:], [r3], [128, TC], cond=T0)
        tap("l_mh", T1[:, 0:TC], [r1, r1h], [128, TC], cond=T0)
        P.add("dve", lambda h: h.scalar_tensor_tensor(out=T4[:], in0=T4[:], scalar=1.0, in1=T2[:],
                                                      op0=ALU.add, op1=ALU.mult),
              reads=[r4, r2], writes=[r4], dur=T_DVE_TT)
        P.add("pool", lambda h: h.tensor_tensor(out=T4[:], in0=T4[:], in1=T1[:, 0:TC], op=ALU.mult),
              reads=[r4, r1, r1h], writes=[r4], dur=T_POOL_TT)
        tap("l_drive", T4[:], [r4], [128, TC], cond=T0)
        P.add("dve", lambda h: h.tensor_tensor_scan(out=T2[:], data0=T3[:], data1=T4[:], initial=HS[:, j:j + 1],
                                                    op0=ALU.mult, op1=ALU.add),
              reads=[r3, r4, "HS%d" % j], writes=[r2], dur=T_SCAN)
        P.add("pool", lambda h: h.tensor_copy(out=HS[:, j:j + 1], in_=T2[:, TC - 1:TC]), reads=[r2],
              writes=["HS%d" % j], dur=T_PTINY)
        tap("l_h", T2[:], [r2], [128, TC], cond=T0)
        if SQ_POOL:
            P.add("pool", lambda h: h.tensor_tensor(out=H2[:], in0=T2[:], in1=T2[:], op=ALU.mult), reads=[r2], writes=[rH2],
                  dur=T_POOL_TT)
        else:
            P.add("act", lambda h: h.activation(out=H2[:], in_=T2[:], func=AF.Square), reads=[r2], writes=[rH2], dur=T_ACT)
        ax = ax_bank("ls")
        P.add("pe", lambda h: h.matmul(bank[ax], lhsT=blk[:], rhs=H2[:], start=True, stop=True),
              reads=[rH2, "blk"], writes=[psres(ax)], dur=T_MM)
        P.add("act", lambda h: h.activation(out=T4[:], in_=bank[ax], func=AF.Ln, scale=1.0 / 64, bias=epsb[:]), tbl="E",
              reads=[psres(ax), "epsb"], writes=[r4], dur=T_ACT_P)
        P.add("act", lambda h: h.activation(out=T4[:], in_=T4[:], func=AF.Exp, scale=-0.5), tbl="E",
              reads=[r4], writes=[r4], dur=T_ACT)
        P.add("dve", lambda h: h.scalar_tensor_tensor(out=T2[:], in0=T2[:], scalar=pw(12), in1=T4[:],
                                                      op0=ALU.mult, op1=ALU.mult),
              reads=[r2, r4] + PARR, writes=[r2], dur=T_DVE_TT)
        P.add("pool", lambda h: h.tensor_tensor(out=yT[ys][:, 8 + j, :], in0=T2[:], in1=T5[:], op=ALU.mult),
              reads=[r2, r5], writes=["yT%d.%d" % (ys, 8 + j)], dur=T_POOL_TT)
        tap("l_yT", yT[ys][:, 8 + j, :], ["yT%d.%d" % (ys, 8 + j)], [128, TC], BF16, cond=T0)

    opc = [0]

    def out_tile(g, part=None):
        n, tt = divmod(g, 4)
        ys = n % 2
        if part is None:
            s = g % 2
            XR, rXR, kXR, kST = xr[s], "xr%d" % s, "xrl%d" % s, "st%d" % s
            cs = list(range(16))
        else:
            XR, rXR, kXR, kST = tail_buf(tt)
            lo, hi = TAIL_BOUNDS[part], TAIL_BOUNDS[part + 1]
            cs = [c for c in range(16) if lo <= (c % 8) < hi]
        if part in (None, 0):
            dma("sp", XR[:], x_d[g * 128:(g + 1) * 128, :], [], [rXR], kXR, 524288)
        for half in range(2):
            obs = OPB if part is None else OPB_TAIL
            b = obs[opc[0] % len(obs)]
            opc[0] += 1

            def emit(h, b=b, half=half):
                last = None
                for c in cs:
                    last = h.matmul(bank[b], lhsT=yT[ys][:, c, tt * 128:(tt + 1) * 128],
                                    rhs=wout[:, c, half * 512:(half + 1) * 512], start=(c == cs[0]), stop=(c == cs[-1]))
                return last
            P.add("pe", emit, reads=["yT%d.%d" % (ys, c) for c in cs] + WOUTR, writes=[psres(b)],
                  dur=len(cs) * T_MM)
            P.add("dve", (lambda h, b=b, half=half: h.tensor_tensor(out=XR[:, half * 512:(half + 1) * 512],
                                                                    in0=bank[b], in1=XR[:, half * 512:(half + 1) * 512],
                                                                    op=ALU.add)),
                  reads=[psres(b), rXR], writes=[rXR], dur=T_DVE_PS)
        if part is not None:
            return
        out_norm(g, XR, rXR, kST)

    def tail_buf(tt):
        return [(xr[0], "xr0", "xrl0", "st0"), (xr[1], "xr1", "xrl1", "st1"),
                (xin[0], "xin0", "xin0", "st2"), (xin[1], "xin1", "xin1", "st3")][tt]

    def out_norm(g, XR, rXR, kST):
        s = g % 2
        P.add("act", lambda h: h.activation(out=junk[:], in_=XR[:], func=AF.Square, accum_out=st_o[s][:, 0:1]),
              reads=[rXR], writes=["junk", "sto%d" % s], dur=1.15)
        if g >= (NT - 1) * 4 and TAIL_OPT and NT >= 3:
            P.add("act", lambda h: h.activation(out=st_o[s][:, 1:2], in_=st_o[s][:, 0:1], func=AF.Ln, scale=1.0 / D,
                                                bias=epsb[:]), tbl="E",
                  reads=["sto%d" % s, "epsb"], writes=["sto%db" % s], dur=0.25)
            P.add("act", lambda h: h.activation(out=st_o[s][:, 2:3], in_=st_o[s][:, 1:2], func=AF.Exp, scale=-0.5), tbl="E",
                  reads=["sto%db" % s], writes=["sto%dc" % s], dur=0.25)
        else:
            P.add("dve", lambda h: h.tensor_scalar(out=st_o[s][:, 1:2], in0=st_o[s][:, 0:1], scalar1=1.0 / D,
                                                   scalar2=EPS, op0=ALU.mult, op1=ALU.add),
                  reads=["sto%d" % s], writes=["sto%db" % s], dur=T_TINY)
            P.add("pool", lambda h: h.tensor_tensor(out=st_o[s][:, 2:3], in0=st_o[s][:, 1:2], in1=mhalf[:], op=ALU.pow),
                  reads=["sto%db" % s, "mhalf"], writes=["sto%dc" % s], dur=0.5)
        P.add("dve", lambda h: h.scalar_tensor_tensor(out=XR[:], in0=XR[:], scalar=st_o[s][:, 2:3], in1=fgb[:],
                                                      op0=ALU.mult, op1=ALU.mult),
              reads=[rXR, "sto%dc" % s, "fgb"], writes=[rXR], dur=1.25)
        dma("sp", out_d[g * 128:(g + 1) * 128, :], XR[:], [rXR], [], kST, 524288, final=True)

    for tt in range(4):
        in_tile(tt)
    tap("xnT", xnT[0][:], ["xnT0.%d" % t for t in range(4)], [128, NK, TC], BF16)
    for n in range(NT):
        for j in range(8):
            wslot, parts = load_group(n, j)
            if n == WOUT_N and j < 4:
                load_wout(j)
            if n + 1 < NT and j in IN_J:
                in_tile((n + 1) * 4 + IN_J.index(j))
            if n == NT - 1 and TAIL_OPT and NT >= 3:
                lru_chunk(n, j, wslot, parts)
                conv_chunk(n, j, wslot, parts)
                for pi in range(len(TAIL_BOUNDS) - 2):
                    if j == min(7, TAIL_BOUNDS[pi + 1] - 1 + TAIL_LAG) and pi not in tail_done:
                        tail_done.append(pi)
                        for tt in range(4):
                            out_tile(n * 4 + tt, part=pi)
            else:
                conv_chunk(n, j, wslot, parts)
                lru_chunk(n, j, wslot, parts)
            if n == 1 and WOUT_N == 1:
                if j >= 4:
                    out_tile(j - 4)
            elif n == NT - 1 and TAIL_OPT and NT >= 3:
                if j < 4:
                    out_tile((n - 1) * 4 + j)
            elif n > 0 and j in OUT_J:
                out_tile((n - 1) * 4 + OUT_J.index(j))
    if TAIL_OPT and NT >= 3:
        last = len(TAIL_BOUNDS) - 2
        for pi in range(last + 1):
            if pi not in tail_done:
                for tt in range(4):
                    out_tile((NT - 1) * 4 + tt, part=pi)
        for tt in range(4):
            XR, rXR, kXR, kST = tail_buf(tt)
            out_norm((NT - 1) * 4 + tt, XR, rXR, kST)
    else:
        for tt in range(4):
            out_tile((NT - 1) * 4 + tt)

    P.schedule()
    handles = {"pe": nc.tensor, "act": nc.scalar, "dve": nc.vector, "pool": nc.gpsimd, "sp": nc.sync}
    P.emit(handles, sems, get_dma_sem)
    return nc, es, P


COL_BASE = (2048, 1024, 3072, 0, 4096, 5120)


def relayout_w_in(w):
    w4 = w.reshape(NK, 128, 6, 8, 128)
    order = [b // 1024 for b in COL_BASE]
    w5 = w4[:, :, order]
    return np.ascontiguousarray(w5.transpose(3, 1, 0, 2, 4).reshape(8, 128, NK * 768))


def host_inputs(inputs, S=4096):
    f32 = np.float32
    par = np.concatenate([
        np.asarray(inputs["conv_w"], f32).reshape(3, D),
        np.asarray(inputs["lru_conv_w"], f32).reshape(4, D),
        np.asarray(inputs["lru_conv_b"], f32).reshape(1, D),
        np.asarray(inputs["b_a"], f32).reshape(1, D),
        np.asarray(inputs["b_i"], f32).reshape(1, D),
        np.asarray(inputs["lam"], f32).reshape(1, D),
        np.asarray(inputs["conv_out_g"], f32).reshape(1, D),
        np.asarray(inputs["lru_out_g"], f32).reshape(1, D),
        np.zeros((1, D), f32),
    ], axis=0)
    blk = np.zeros((128, 128), f32)
    blk[:64, :64] = 1.0
    blk[64:, 64:] = 1.0
    shared = {
        "w_in_r": relayout_w_in(np.asarray(inputs["w_in"], f32)),
        "w_out_r": np.ascontiguousarray(np.asarray(inputs["w_out"], f32).reshape(16, 128, D).transpose(1, 0, 2)
                                        .reshape(128, 16 * D)),
        "w_a": np.ascontiguousarray(np.asarray(inputs["w_a"], f32)),
        "w_i": np.ascontiguousarray(np.asarray(inputs["w_i"], f32)),
        "par": np.ascontiguousarray(par),
        "ln_g": np.asarray(inputs["ln_g"], f32).reshape(1, D),
        "final_g": np.asarray(inputs["final_g"], f32).reshape(1, D),
        "ident_bf": np.eye(128, dtype=f32).astype(ml_dtypes.bfloat16),
        "ident_f": np.eye(128, dtype=f32),
        "ones_bf": np.ones((128, 128), f32).astype(ml_dtypes.bfloat16),
        "blk_bf": blk.astype(ml_dtypes.bfloat16),
    }
    return shared


def kernel(**inputs):
    x = np.asarray(inputs["x"], np.float32)
    B, S, _ = x.shape
    shared = host_inputs(inputs)
    nc, es, P = build_program(NT=S // TC)
    in_maps = []
    for b in range(B):
        m = dict(shared)
        m["x"] = np.ascontiguousarray(x[b])
        in_maps.append(m)
    res = run_bass_kernel_spmd(nc, in_maps, core_ids=list(range(B)))
    out = np.stack([np.asarray(r["out"], np.float32) for r in res.results], axis=0)
    return out
```

```python
import math
from contextlib import ExitStack

import numpy as np
import ml_dtypes
import concourse.bass as bass
import concourse.mybir as mybir
from concourse.bass_utils import run_bass_kernel_spmd

F32 = mybir.dt.float32
BF16 = mybir.dt.bfloat16
AF = mybir.ActivationFunctionType
ALU = mybir.AluOpType

ENGS = ("pe", "act", "dve", "pool", "sp")
TBL_PEN = 1.0
JITTER_SEED = 3
EVAC_PRIO = 1
EVAC_RES = set()


class Op:
    __slots__ = ("idx", "eng", "emit", "reads", "writes", "dur", "deps", "kind",
                 "dma_key", "lat", "name", "signals", "count", "fin", "start", "tbl", "prio")

    def __init__(self):
        self.deps = set()
        self.signals = False
        self.count = 0


class Prog:
    def __init__(self):
        self.ops = []
        self.last_w = {}
        self.readers = {}
        self.final_dma = []

    def add(self, eng, emit, reads=(), writes=(), dur=0.3, kind="c", dma_key=None,
            lat=0.0, name="", final=False, tbl=None):
        op = Op()
        op.tbl = tbl
        op.prio = 1
        op.idx = len(self.ops)
        op.eng, op.emit, op.dur, op.kind = eng, emit, dur, kind
        op.dma_key, op.lat, op.name = dma_key, lat, name
        op.reads, op.writes = tuple(reads), tuple(writes)
        if eng != "pe" and kind != "dma" and any(r in EVAC_RES for r in op.reads):
            op.prio = 0
        if kind == "dma":
            op.prio = 0
        for r in op.reads:
            w = self.last_w.get(r)
            if w is not None:
                op.deps.add(w)
        for r in op.writes:
            w = self.last_w.get(r)
            if w is not None:
                op.deps.add(w)
            for rd in self.readers.get(r, ()):
                op.deps.add(rd)
        for r in op.reads:
            self.readers.setdefault(r, []).append(op.idx)
        for r in op.writes:
            self.last_w[r] = op.idx
            self.readers[r] = []
        op.deps.discard(op.idx)
        self.ops.append(op)
        if final:
            self.final_dma.append(op.idx)
        return op

    def schedule(self, sync_lat=0.30):
        ops = self.ops
        n = len(ops)
        if JITTER_SEED:
            import random
            rng = random.Random(JITTER_SEED)
            for o in ops:
                o.dur *= 1.0 + 0.08 * (rng.random() - 0.5)
        succ = [[] for _ in range(n)]
        npred = [0] * n
        for o in ops:
            npred[o.idx] = len(o.deps)
            for d in o.deps:
                succ[d].append(o.idx)
        eng_free = {e: 0.0 for e in ENGS}
        avail = {e: {} for e in ENGS}
        dma_free = {e: 0.0 for e in ENGS}
        for o in ops:
            o.fin = None
            if npred[o.idx] == 0:
                avail[o.eng][o.idx] = 0.0
        order = {e: [] for e in ENGS}
        done = 0
        cur_tbl = None
        self.nswitch = 0
        while done < n:
            best = None
            for e in ENGS:
                a = avail[e]
                if not a:
                    continue
                ef = eng_free[e]
                if e == "act":
                    c = min((max(rt, ef) + (TBL_PEN if (ops[i].tbl is not None and ops[i].tbl != cur_tbl) else 0.0),
                             ops[i].prio * EVAC_PRIO, i) for i, rt in a.items())
                else:
                    c = min((max(rt, ef), ops[i].prio * EVAC_PRIO, i) for i, rt in a.items())
                if best is None or c < best[0]:
                    best = (c, e)
            (st, _pr, idx), e = best
            del avail[e][idx]
            o = ops[idx]
            if e == "act" and o.tbl is not None and o.tbl != cur_tbl:
                st = st - TBL_PEN + 1.3
                cur_tbl = o.tbl
                self.nswitch += 1
            o.start = st
            eng_free[e] = st + o.dur
            if o.kind == "dma":
                t0 = max(st + o.dur, dma_free[e])
                dma_free[e] = t0 + o.lat
                o.fin = t0 + o.lat + 2.0
            else:
                o.fin = st + o.dur
            order[e].append(idx)
            done += 1
            for s in succ[idx]:
                npred[s] -= 1
                if npred[s] == 0:
                    so = ops[s]
                    rt = 0.0
                    for d in so.deps:
                        rt = max(rt, ops[d].fin + sync_lat)
                    avail[so.eng][s] = rt
        self.order = order
        self.makespan = max(o.fin for o in ops)
        self.busy = {e: sum(ops[i].dur for i in order[e]) for e in ENGS}
        return order

    def emit(self, handles, sems, get_dma_sem):
        ops = self.ops
        order = self.order
        for o in ops:
            for d in o.deps:
                p = ops[d]
                if p.eng == "pe" and o.eng == "pe" and p.kind != "dma" and o.kind != "dma":
                    continue
                p.signals = True
        for i in self.final_dma:
            ops[i].signals = True
        cnt = {e: 0 for e in ENGS}
        dcnt = {}
        for e in ENGS:
            for idx in order[e]:
                o = ops[idx]
                if o.kind == "dma":
                    dcnt[o.dma_key] = dcnt.get(o.dma_key, 0) + 1
                    o.count = dcnt[o.dma_key] * 16
                    o.signals = True
                elif o.signals:
                    cnt[e] += 1
                    o.count = cnt[e]
        nwait = 0
        for e in ENGS:
            h = handles[e]
            waited = {}
            for idx in order[e]:
                o = ops[idx]
                need = {}
                for d in o.deps:
                    p = ops[d]
                    if p.kind == "dma":
                        key = ("d", p.dma_key)
                        sem = get_dma_sem(p.dma_key)
                    else:
                        if p.eng == "pe" and e == "pe" and o.kind != "dma":
                            continue
                        key = ("e", p.eng)
                        sem = sems[p.eng]
                    if need.get(key, (None, 0))[1] < p.count:
                        need[key] = (sem, p.count)
                for key, (sem, val) in need.items():
                    if waited.get(key, 0) < val:
                        h.wait_ge(sem, val)
                        waited[key] = val
                        nwait += 1
                inst = o.emit(h)
                if o.signals:
                    if o.kind == "dma":
                        inst.then_inc(get_dma_sem(o.dma_key), 16)
                    else:
                        inst.then_inc(sems[e], 1)
        h = handles["sp"]
        fin = {}
        for i in self.final_dma:
            o = ops[i]
            fin[o.dma_key] = max(fin.get(o.dma_key, 0), o.count)
        for k, v in fin.items():
            h.wait_ge(get_dma_sem(k), v)
        self.nwait = nwait


D = 1024
NK = 8
TC = 512
EPS = 1e-6
NPAR = 14
NCS = 2
NLS = 3
NWS = 3
NXIN = 3
SPLIT = (8, 8)
WOUT_N = 1
PJB = (0, 1, 2, 3, 6)
SQ_POOL = False
LATE_G = True
IN_J = (0, 1, 2, 3)
OUT_J = (3, 4, 5, 6)
TAIL_OPT = True
TAIL_BOUNDS = (0, 4, 6, 7, 8)
TAIL_LAG = 2
OPB_TAIL = (7, 6, 0, 1, 2, 3)
BANKS = {"gr": 4, "gi": 4, "cs": 4, "ls": 5}
OPB = (7,)
LN_HALF = math.log(0.5)

T_MM = 0.222
T_ACT = 0.6
T_ACT_P = 0.56
T_DVE_TT = 0.79
T_DVE_TS = 0.58
T_DVE_PS = 0.65
T_SCAN = 1.25
T_POOL_TT = 1.4
T_POOL_CP = 0.75
T_TINY = 0.15
T_PTINY = 0.44


def build_program(NT=8, taps=False):
    S = NT * TC
    NTT = S // 128
    nc = bass.Bass("TRN2", target_bir_lowering=False)
    dr = lambda name, shape, dt, kind="ExternalInput": nc.dram_tensor(name, shape, dt, kind=kind).ap()
    x_d = dr("x", [S, D], F32)
    win_d = dr("w_in_r", [8, 128, NK * 768], F32)
    wout_d = dr("w_out_r", [128, 16 * D], F32)
    wa_d = dr("w_a", [16, 64, 64], F32)
    wi_d = dr("w_i", [16, 64, 64], F32)
    par_d = dr("par", [NPAR, D], F32)
    lng_d = dr("ln_g", [1, D], F32)
    fg_d = dr("final_g", [1, D], F32)
    idb_d = dr("ident_bf", [128, 128], BF16)
    idf_d = dr("ident_f", [128, 128], F32)
    ones_d = dr("ones_bf", [128, 128], BF16)
    blk_d = dr("blk_bf", [128, 128], BF16)
    out_d = dr("out", [S, D], F32, kind="ExternalOutput")
    wsc_d = dr("wsc", [8, 128, NK * 768], BF16, kind="Internal")

    es = ExitStack()
    sb = lambda name, shape, dt: es.enter_context(nc.sbuf_tensor(name, shape, dt))
    wbuf = [sb("wbuf%d" % i, [128, NK, 768], BF16) for i in range(NWS)]
    wout = sb("wout", [128, 16, D], BF16)
    gw = sb("gw", [128, 2, 8, 128], BF16)
    idb = sb("idb", [128, 128], BF16)
    idf = sb("idf", [128, 128], F32)
    ones = sb("ones", [128, 128], BF16)
    blk = sb("blk", [128, 128], BF16)
    lng = sb("lng", [128, D], F32)
    fgb = sb("fgb", [128, D], F32)
    PAR = sb("PAR", [128, 8, 16], F32)
    xin = [sb("xin%d" % i, [128, D], F32) for i in range(NXIN)]
    xr = [sb("xr%d" % i, [128, D], F32) for i in range(2)]
    xn = [sb("xn%d" % i, [128, D], BF16) for i in range(2)]
    junk = sb("junk", [128, D], BF16)
    st_i = [sb("sti%d" % i, [128, 4], F32) for i in range(2)]
    st_o = [sb("sto%d" % i, [128, 4], F32) for i in range(2)]
    mhalf = sb("mhalf", [128, 1], F32)
    xnT = [sb("xnT%d" % i, [128, NK, TC], BF16) for i in range(2)]
    yT = [sb("yT%d" % i, [128, 16, TC], BF16) for i in range(2)]
    cA = [sb("cA%d" % i, [128, TC + 2], F32) for i in range(NCS)]
    cB = [sb("cB%d" % i, [128, TC], F32) for i in range(NCS)]
    cSG = [sb("cSG%d" % i, [128, TC], F32) for i in range(NCS)]
    cY2 = [sb("cY2%d" % i, [128, TC], BF16) for i in range(NCS)]
    l1 = [sb("l1_%d" % i, [128, TC + 3], F32) for i in range(NLS)]
    l2 = [sb("l2_%d" % i, [128, TC], F32) for i in range(NLS)]
    l3 = [sb("l3_%d" % i, [128, TC], F32) for i in range(NLS)]
    l4 = [sb("l4_%d" % i, [128, TC], F32) for i in range(NLS)]
    l5 = [sb("l5_%d" % i, [128, TC], F32) for i in range(NLS)]
    lUB = [sb("lUB%d" % i, [128, TC], BF16) for i in range(NLS)]
    lH2 = [sb("lH2%d" % i, [128, TC], BF16) for i in range(NLS)]
    HC = sb("HC", [128, 8, 2], F32)
    HL = sb("HL", [128, 8, 4], F32)
    HS = sb("HS", [128, 8], F32)
    ps = es.enter_context(nc.psum_tensor("ps", [128, 7, TC], F32))
    psT = es.enter_context(nc.psum_tensor("psT", [128, 2 * TC], BF16))
    bank = [ps[:, b, :] for b in range(7)] + [psT[:].bitcast(F32)]

    sems = {e: es.enter_context(nc.semaphore("s_" + e)) for e in ("pe", "act", "dve", "pool")}
    dsems = {}

    def get_dma_sem(k):
        if k not in dsems:
            dsems[k] = es.enter_context(nc.semaphore("d_" + k))
        return dsems[k]

    P = Prog()
    tapn = [0]
    psres = lambda b: "psT" if b == 7 else "ps%d" % b

    def tap(name, ap, res, shape, dt=F32, cond=True):
        if not (taps and cond):
            return
        t = nc.dram_tensor("tap_" + name, list(shape), dt, kind="ExternalOutput").ap()
        tapn[0] += 1
        P.add("sp", (lambda h: h.dma_start(out=t, in_=ap)), reads=list(res), writes=[], kind="dma",
              dma_key="tap%d" % tapn[0], lat=1.0, dur=0.1, final=True)

    def dma(q, out, in_, reads, writes, key, nbytes, final=False, name="", **kw):
        return P.add(q, (lambda h: h.dma_start(out=out, in_=in_, **kw)), reads=reads, writes=writes,
                     kind="dma", dma_key=key, lat=nbytes / 220e3, dur=(1.5 if q == "pool" else 0.1),
                     final=final, name=name)

    epsb = sb("epsb", [128, 1], F32)
    oneb = sb("oneb", [128, 1], F32)
    lnhb = sb("lnhb", [128, 1], F32)
    P.add("pool", lambda h: h.memset(epsb[:], EPS), writes=["epsb"], dur=T_TINY)
    P.add("pool", lambda h: h.memset(oneb[:], 1.0), writes=["oneb"], dur=T_TINY)
    P.add("pool", lambda h: h.memset(lnhb[:], LN_HALF), writes=["lnhb"], dur=T_TINY)

    bc = lambda t: bass.AP(t.tensor, 0, [[0, 128], [1, D]])
    dma("sp", idb[:], idb_d, [], ["idb"], "c0", 32768)
    dma("sp", idf[:], idf_d, [], ["idf"], "c1", 65536)
    dma("sp", xr[0][0:NPAR, :], par_d, [], ["xr0"], "c4", NPAR * 4096)
    for g0, (XB0, rX0, kX0) in enumerate([(xin[0], "xin0", "xin0"), (xin[1], "xin1", "xin1"), (xin[2], "xin2", "xin2"),
                                          (xr[1], "xr1", "xrl1")]):
        dma("sp", XB0[:], x_d[g0 * 128:(g0 + 1) * 128, :], [], [rX0], kX0, 524288)
        if g0 == 0:
            dma("sp", lng[:], bc(lng_d), [], ["lng"], "c5", 524288)

    dma("sp", ones[:], ones_d, [], ["ones"], "c2", 32768)
    dma("sp", blk[:], blk_d, [], ["blk"], "c3", 32768)
    dma("sp", fgb[:], bc(fg_d), [], ["fgb"], "c6", 524288)
    P.add("pool", lambda h: h.memset(mhalf[:], -0.5), writes=["mhalf"], dur=T_TINY)
    P.add("pool", lambda h: h.memset(HC[:], 0.0), writes=["HC%d" % j for j in range(8)], dur=T_TINY)
    P.add("pool", lambda h: h.memset(HL[:], 0.0), writes=["HL%d" % j for j in range(8)], dur=T_TINY)
    P.add("pool", lambda h: h.memset(HS[:], 0.0), writes=["HS%d" % j for j in range(8)], dur=T_TINY)
    P.add("pool", lambda h: h.memset(gw[:], 0.0), writes=["gw"], dur=1.0)

    def emit_par_T(h):
        last = None
        for j in range(8):
            last = h.transpose(out=bank[4][:, j * 16:j * 16 + NPAR], in_=xr[0][0:NPAR, j * 128:(j + 1) * 128],
                               identity=idf[0:NPAR, 0:NPAR])
        return last
    P.add("pe", emit_par_T, reads=["xr0", "idf"], writes=["ps4"], dur=1.0)
    P.add("dve", lambda h: h.tensor_copy(out=PAR[:, :, 0:NPAR],
                                         in_=bank[4][:, 0:128].rearrange("p (j c) -> p j c", c=16)[:, :, 0:NPAR]),
          reads=["ps4"], writes=["PAR"], dur=0.3)
    P.add("act", lambda h: h.activation(out=PAR[:, :, 13], in_=PAR[:, :, 10], func=AF.Exp, scale=-1.0), tbl="E",
          reads=["PAR"], writes=["PARd"], dur=0.3)
    P.add("act", lambda h: h.activation(out=PAR[:, :, 13], in_=PAR[:, :, 13], func=AF.Ln, bias=oneb[:]), tbl="E",
          reads=["PARd", "oneb"], writes=["PARd"], dur=0.3)
    P.add("dve", lambda h: h.tensor_scalar(out=PAR[:, :, 13], in0=PAR[:, :, 13], scalar1=-4.0, scalar2=None,
                                           op0=ALU.mult), reads=["PARd"], writes=["PARd"], dur=0.2)
    P.add("dve", lambda h: h.tensor_scalar(out=PAR[:, :, 14:16], in0=PAR[:, :, 8:10], scalar1=0.5, scalar2=None,
                                           op0=ALU.mult), reads=["PAR"], writes=["PARe"], dur=0.2)
    PARR = ["PAR", "PARd", "PARe"]
    tap("PAR", PAR[:], PARR, [128, 8, 16])

    for gi, wd in enumerate((wa_d, wi_d)):
        src = wd.rearrange("(c two) d e -> two d c e", two=2)
        for two in range(2):
            dma("pool", gw[two * 64:(two + 1) * 64, gi, :, two * 64:(two + 1) * 64], src[two],
                ["gw"], ["gw%d%d" % (gi, two)], "gw%d%d" % (gi, two), 131072)
    GWR = ["gw%d%d" % (a, b) for a in range(2) for b in range(2)]

    wcount = [0]
    late_ok = [False]

    def load_group(n, j):
        slot = wcount[0] % NWS
        wcount[0] += 1
        parts = ["wb%d.%d" % (slot, s) for s in range(6)]
        ready_n = 1 if j < SPLIT[0] else (2 if j < SPLIT[1] else 3)
        cast_load = n < ready_n
        if cast_load:
            hold = ["xr1"] if (n == 0 and j in (1, 2)) else (["xin1"] if (n == 0 and j == 0) else [])
            dma("pool", wbuf[slot][:].rearrange("p k e -> p (k e)").rearrange("p (a b) -> p a b", b=2048),
                win_d[j].rearrange("p (a b) -> p a b", b=2048), hold, parts, "wb%d" % slot, 128 * 6144 * 4)
            if NT > n + 1 and n == ready_n - 1:
                dma("sp", wsc_d[j], wbuf[slot][:].rearrange("p k e -> p (k e)"), parts, ["wsc%d" % j],
                    "ws%d" % slot, 1572864)
        else:
            dma("sp", wbuf[slot][:].rearrange("p k e -> p (k e)"), wsc_d[j], ["wsc%d" % j], parts,
                "wl%d" % slot, 1572864)
        late_ok[0] = not cast_load
        return slot, parts

    def load_wout(q):
        src = wout_d.rearrange("p (c d) -> p c d", d=D)
        for hh in range(2):
            dma("pool", wout[:, q * 4 + 2 * hh:q * 4 + 2 * hh + 2, :], src[:, q * 4 + 2 * hh:q * 4 + 2 * hh + 2, :],
                ["wsc7"], ["wout%d.%d" % (q, hh)], "wo%d%d" % (q, hh), 1048576)
    WOUTR = ["wout%d.%d" % (q, hh) for q in range(4) for hh in range(2)]

    def in_tile(g):
        n, tt = divmod(g, 4)
        ns = n % 2
        if g == 3:
            XB, rX, kX = xr[1], "xr1", "xrl1"
        else:
            s = (g if g < 3 else g - 1) % NXIN
            XB, rX, kX = xin[s], "xin%d" % s, "xin%d" % s
        if g >= 4:
            dma("sp", XB[:], x_d[g * 128:(g + 1) * 128, :], [], [rX], kX, 524288)
        s2 = g % 2
        P.add("act", lambda h: h.activation(out=junk[:], in_=XB[:], func=AF.Square, accum_out=st_i[s2][:, 0:1]),
              reads=[rX], writes=["junk", "sti%d" % s2], dur=1.15)
        P.add("dve", lambda h: h.tensor_scalar(out=st_i[s2][:, 1:2], in0=st_i[s2][:, 0:1], scalar1=1.0 / D,
                                               scalar2=EPS, op0=ALU.mult, op1=ALU.add),
              reads=["sti%d" % s2], writes=["sti%db" % s2], dur=T_TINY)
        P.add("pool", lambda h: h.tensor_tensor(out=st_i[s2][:, 2:3], in0=st_i[s2][:, 1:2], in1=mhalf[:], op=ALU.pow),
              reads=["sti%db" % s2, "mhalf"], writes=["sti%dc" % s2], dur=0.5)
        P.add("dve", lambda h: h.scalar_tensor_tensor(out=xn[s2][:], in0=XB[:], scalar=st_i[s2][:, 2:3], in1=lng[:],
                                                      op0=ALU.mult, op1=ALU.mult),
              reads=[rX, "sti%dc" % s2, "lng"], writes=["xn%d" % s2], dur=1.25)
        def emit_T(h):
            last = None
            for k in range(NK):
                last = h.transpose(out=psT[:, k * 128:(k + 1) * 128], in_=xn[s2][:, k * 128:(k + 1) * 128],
                                   identity=idb[:])
            return last
        P.add("pe", emit_T, reads=["xn%d" % s2, "idb"], writes=["psT"], dur=0.9)
        P.add("dve", lambda h: h.tensor_copy(out=xnT[ns][:, :, tt * 128:(tt + 1) * 128],
                                             in_=psT[:].rearrange("p (k t) -> p k t", t=128)),
              reads=["psT"], writes=["xnT%d.%d" % (ns, tt)], dur=0.75)

    axc = [0]
    tail_done = []
    pjc = [0]
    ccnt = [0]
    lcnt = [0]

    def pj_bank():
        b = PJB[pjc[0] % len(PJB)]
        pjc[0] += 1
        EVAC_RES.add("ps%d" % b)
        return b

    def ax_bank(kind):
        return BANKS[kind]

    def proj(n, wslot, wres, col0, b):
        ns = n % 2

        def emit(h):
            last = None
            for k in range(NK):
                last = h.matmul(bank[b], lhsT=wbuf[wslot][:, k, col0:col0 + 128], rhs=xnT[ns][:, k, :],
                                start=(k == 0), stop=(k == NK - 1))
            return last
        P.add("pe", emit, reads=[wres] + ["xnT%d.%d" % (ns, t) for t in range(4)], writes=["ps%d" % b],
              dur=NK * T_MM)

    def conv_chunk(n, j, wslot, parts):
        sc = ccnt[0] % NCS
        ccnt[0] += 1
        ys = n % 2
        A, Bt, SG, Y2 = cA[sc], cB[sc], cSG[sc], cY2[sc]
        rA, rAh, rB, rSG, rY2 = "cA%d" % sc, "cAh%d" % sc, "cB%d" % sc, "cSG%d" % sc, "cY2%d" % sc
        pw = lambda c: PAR[:, j, c:c + 1]
        b_xc = pj_bank()
        proj(n, wslot, parts[0], 0, b_xc)
        P.add("act", lambda h: h.activation(out=Bt[:], in_=bank[b_xc], func=AF.Copy),
              reads=["ps%d" % b_xc], writes=[rB], dur=T_ACT_P)
        T0 = (n == 0 and j == 0)
        tap("c_xc", Bt[:], [rB], [128, TC], cond=T0)
        P.add("pool", lambda h: h.tensor_copy(out=A[:, 0:2], in_=HC[:, j, :]), reads=["HC%d" % j], writes=[rAh],
              dur=T_PTINY)
        b_c = pj_bank()
        proj(n, wslot, parts[1], 128, b_c)
        P.add("dve", lambda h: h.tensor_tensor(out=A[:, 2:TC + 2], in0=bank[b_c], in1=Bt[:], op=ALU.mult),
              reads=["ps%d" % b_c, rB], writes=[rA], dur=T_DVE_PS)
        P.add("pool", lambda h: h.tensor_copy(out=HC[:, j, :], in_=A[:, TC:TC + 2]), reads=[rA], writes=["HC%d" % j],
              dur=T_PTINY)
        tap("c_cx", A[:], [rA, rAh], [128, TC + 2], cond=T0)
        def do_g():
            b_g = pj_bank()
            proj(n, wslot, parts[2], 256, b_g)
            P.add("act", lambda h: h.activation(out=SG[:], in_=bank[b_g], func=AF.Silu), tbl="B",
                  reads=["ps%d" % b_g], writes=[rSG], dur=T_ACT_P)
            tap("c_sg", SG[:], [rSG], [128, TC], cond=T0)
        LG = LATE_G and late_ok[0]
        if not LG:
            do_g()
        P.add("dve", lambda h: h.tensor_scalar(out=Bt[:], in0=A[:, 2:TC + 2], scalar1=pw(2), scalar2=None, op0=ALU.mult),
              reads=[rA] + PARR, writes=[rB], dur=T_DVE_TS)
        P.add("dve", lambda h: h.scalar_tensor_tensor(out=Bt[:], in0=A[:, 1:TC + 1], scalar=pw(1), in1=Bt[:],
                                                      op0=ALU.mult, op1=ALU.add),
              reads=[rA, rAh, rB] + PARR, writes=[rB], dur=T_DVE_TT)
        P.add("dve", lambda h: h.scalar_tensor_tensor(out=Bt[:], in0=A[:, 0:TC], scalar=pw(0), in1=Bt[:],
                                                      op0=ALU.mult, op1=ALU.add),
              reads=[rA, rAh, rB] + PARR, writes=[rB], dur=T_DVE_TT)
        tap("c_cv", Bt[:], [rB], [128, TC], cond=T0)
        b_b = pj_bank()
        proj(n, wslot, parts[3], 384, b_b)
        P.add("dve", lambda h: h.tensor_tensor(out=Bt[:], in0=bank[b_b], in1=Bt[:], op=ALU.mult),
              reads=["ps%d" % b_b, rB], writes=[rB], dur=T_DVE_PS)
        tap("c_y", Bt[:], [rB], [128, TC], cond=T0)
        if LG:
            do_g()
        if SQ_POOL:
            P.add("pool", lambda h: h.tensor_tensor(out=Y2[:], in0=Bt[:], in1=Bt[:], op=ALU.mult), reads=[rB], writes=[rY2],
                  dur=T_POOL_TT)
        else:
            P.add("act", lambda h: h.activation(out=Y2[:], in_=Bt[:], func=AF.Square), reads=[rB], writes=[rY2], dur=T_ACT)
        ax = ax_bank("cs")
        P.add("pe", lambda h: h.matmul(bank[ax], lhsT=ones[:], rhs=Y2[:], start=True, stop=True),
              reads=[rY2, "ones"], writes=[psres(ax)], dur=T_MM)
        P.add("act", lambda h: h.activation(out=A[:, 0:TC], in_=bank[ax], func=AF.Ln, scale=1.0 / 128, bias=epsb[:]), tbl="E",
              reads=[psres(ax), "epsb"], writes=[rA, rAh], dur=T_ACT_P)
        P.add("act", lambda h: h.activation(out=A[:, 0:TC], in_=A[:, 0:TC], func=AF.Exp, scale=-0.5), tbl="E",
              reads=[rA, rAh], writes=[rA, rAh], dur=T_ACT)
        tap("c_rstd", A[:, 0:TC], [rA, rAh], [128, TC], cond=T0)
        P.add("dve", lambda h: h.scalar_tensor_tensor(out=Bt[:], in0=Bt[:], scalar=pw(11), in1=A[:, 0:TC],
                                                      op0=ALU.mult, op1=ALU.mult),
              reads=[rB, rA, rAh] + PARR, writes=[rB], dur=T_DVE_TT)
        P.add("pool", lambda h: h.tensor_tensor(out=yT[ys][:, j, :], in0=Bt[:], in1=SG[:], op=ALU.mult),
              reads=[rB, rSG], writes=["yT%d.%d" % (ys, j)], dur=T_POOL_TT)
        tap("c_yT", yT[ys][:, j, :], ["yT%d.%d" % (ys, j)], [128, TC], BF16, cond=T0)

    def lru_chunk(n, j, wslot, parts):
        sl = lcnt[0] % NLS
        lcnt[0] += 1
        ys = n % 2
        T1, T2, T3, T4, T5, UB, H2 = l1[sl], l2[sl], l3[sl], l4[sl], l5[sl], lUB[sl], lH2[sl]
        r1, r1h, r2, r3, r4, r5, rUB, rH2 = ["l%s_%d" % (t, sl) for t in ("1", "1h", "2", "3", "4", "5", "UB", "H2")]
        pw = lambda c: PAR[:, j, c:c + 1]
        b_x = pj_bank()
        proj(n, wslot, parts[4], 512, b_x)
        P.add("act", lambda h: h.activation(out=T1[:, 3:TC + 3], in_=bank[b_x], func=AF.Copy),
              reads=["ps%d" % b_x], writes=[r1], dur=T_ACT_P)
        P.add("pool", lambda h: h.tensor_copy(out=T1[:, 0:3], in_=HL[:, j, 0:3]), reads=["HL%d" % j], writes=[r1h],
              dur=T_PTINY)
        P.add("pool", lambda h: h.tensor_copy(out=HL[:, j, 0:3], in_=T1[:, TC:TC + 3]), reads=[r1], writes=["HL%d" % j],
              dur=T_PTINY)
        def do_gl():
            b_g = pj_bank()
            proj(n, wslot, parts[5], 640, b_g)
            P.add("act", lambda h: h.activation(out=T5[:], in_=bank[b_g], func=AF.Silu), tbl="B",
                  reads=["ps%d" % b_g], writes=[r5], dur=T_ACT_P)
        LG = LATE_G and late_ok[0]
        if not LG:
            do_gl()
        P.add("dve", lambda h: h.tensor_scalar(out=T2[:], in0=T1[:, 3:TC + 3], scalar1=pw(6), scalar2=pw(7),
                                               op0=ALU.mult, op1=ALU.add),
              reads=[r1] + PARR, writes=[r2], dur=T_DVE_TS)
        for k, eng in ((1, "dve"), (2, "dve"), (3, "dve")):
            P.add(eng, (lambda h, k=k: h.scalar_tensor_tensor(out=T2[:], in0=T1[:, 3 - k:TC + 3 - k], scalar=pw(6 - k),
                                                              in1=T2[:], op0=ALU.mult, op1=ALU.add)),
                  reads=[r1, r1h, r2] + PARR, writes=[r2], dur=(T_DVE_TT if eng == "dve" else T_POOL_TT))
        T0 = (n == 0 and j == 0)
        tap("l_xl", T1[:], [r1, r1h], [128, TC + 3], cond=T0)
        tap("l_u", T2[:], [r2], [128, TC], cond=T0)
        P.add("dve", lambda h: h.tensor_copy(out=UB[:], in_=T2[:]), reads=[r2], writes=[rUB], dur=T_DVE_TS)
        ax_r = ax_bank("gr")
        P.add("pe", lambda h: h.matmul(bank[ax_r], lhsT=gw[:, 0, j, :], rhs=UB[:], start=True, stop=True),
              reads=[rUB] + GWR, writes=[psres(ax_r)], dur=T_MM)
        P.add("act", lambda h: h.activation(out=T3[:], in_=bank[ax_r], func=AF.Tanh, scale=0.5, bias=pw(14)), tbl="B",
              reads=[psres(ax_r)] + PARR, writes=[r3], dur=T_ACT_P + 0.1)
        ax_i = ax_bank("gi")
        P.add("pe", lambda h: h.matmul(bank[ax_i], lhsT=gw[:, 1, j, :], rhs=UB[:], start=True, stop=True),
              reads=[rUB] + GWR, writes=[psres(ax_i)], dur=T_MM)
        P.add("act", lambda h: h.activation(out=T4[:], in_=bank[ax_i], func=AF.Tanh, scale=0.5, bias=pw(15)), tbl="B",
              reads=[psres(ax_i)] + PARR, writes=[r4], dur=T_ACT_P + 0.1)
        if LG:
            do_gl()
        tap("l_tr", T3[:], [r3], [128, TC], cond=T0)
        tap("l_ti", T4[:], [r4], [128, TC], cond=T0)
        P.add("act", lambda h: h.activation(out=T3[:], in_=T3[:], func=AF.Exp, scale=pw(13), bias=pw(13)), tbl="E",
              reads=[r3] + PARR, writes=[r3], dur=T_ACT + 0.2)
        P.add("pool", lambda h: h.tensor_tensor(out=T1[:, 0:TC], in0=T3[:], in1=T3[:], op=ALU.mult),
              reads=[r3], writes=[r1, r1h], dur=T_POOL_TT)
        P.add("act", lambda h: h.activation(out=T1[:, 0:TC], in_=T1[:, 0:TC], func=AF.Ln, scale=-1.0, bias=oneb[:]), tbl="E",
              reads=[r1, r1h, "oneb"], writes=[r1, r1h], dur=T_ACT)
        P.add("act", lambda h: h.activation(out=T1[:, 0:TC], in_=T1[:, 0:TC], func=AF.Exp, scale=0.5, bias=lnhb[:]), tbl="E",
              reads=[r1, r1h, "lnhb"], writes=[r1, r1h], dur=T_ACT)
        tap("l_a", T3[:], [r3], [128, TC], cond=T0)
        tap("l_mh", T1[:, 0:TC], [r1, r1h], [128, TC], cond=T0)
        P.add("dve", lambda h: h.scalar_tensor_tensor(out=T4[:], in0=T4[:], scalar=1.0, in1=T2[:],
                                                      op0=ALU.add, op1=ALU.mult),
              reads=[r4, r2], writes=[r4], dur=T_DVE_TT)
        P.add("pool", lambda h: h.tensor_tensor(out=T4[:], in0=T4[:], in1=T1[:, 0:TC], op=ALU.mult),
              reads=[r4, r1, r1h], writes=[r4], dur=T_POOL_TT)
        tap("l_drive", T4[:], [r4], [128, TC], cond=T0)
        P.add("dve", lambda h: h.tensor_tensor_scan(out=T2[:], data0=T3[:], data1=T4[:], initial=HS[:, j:j + 1],
                                                    op0=ALU.mult, op1=ALU.add),
              reads=[r3, r4, "HS%d" % j], writes=[r2], dur=T_SCAN)
        P.add("pool", lambda h: h.tensor_copy(out=HS[:, j:j + 1], in_=T2[:, TC - 1:TC]), reads=[r2],
              writes=["HS%d" % j], dur=T_PTINY)
        tap("l_h", T2[:], [r2], [128, TC], cond=T0)
        if SQ_POOL:
            P.add("pool", lambda h: h.tensor_tensor(out=H2[:], in0=T2[:], in1=T2[:], op=ALU.mult), reads=[r2], writes=[rH2],
                  dur=T_POOL_TT)
        else:
            P.add("act", lambda h: h.activation(out=H2[:], in_=T2[:], func=AF.Square), reads=[r2], writes=[rH2], dur=T_ACT)
        ax = ax_bank("ls")
        P.add("pe", lambda h: h.matmul(bank[ax], lhsT=blk[:], rhs=H2[:], start=True, stop=True),
              reads=[rH2, "blk"], writes=[psres(ax)], dur=T_MM)
        P.add("act", lambda h: h.activation(out=T4[:], in_=bank[ax], func=AF.Ln, scale=1.0 / 64, bias=epsb[:]), tbl="E",
              reads=[psres(ax), "epsb"], writes=[r4], dur=T_ACT_P)
        P.add("act", lambda h: h.activation(out=T4[:], in_=T4[:], func=AF.Exp, scale=-0.5), tbl="E",
              reads=[r4], writes=[r4], dur=T_ACT)
        P.add("dve", lambda h: h.scalar_tensor_tensor(out=T2[:], in0=T2[:], scalar=pw(12), in1=T4[:],
                                                      op0=ALU.mult, op1=ALU.mult),
              reads=[r2, r4] + PARR, writes=[r2], dur=T_DVE_TT)
        P.add("pool", lambda h: h.tensor_tensor(out=yT[ys][:, 8 + j, :], in0=T2[:], in1=T5[:], op=ALU.mult),
              reads=[r2, r5], writes=["yT%d.%d" % (ys, 8 + j)], dur=T_POOL_TT)
        tap("l_yT", yT[ys][:, 8 + j, :], ["yT%d.%d" % (ys, 8 + j)], [128, TC], BF16, cond=T0)

    opc = [0]

    def out_tile(g, part=None):
        n, tt = divmod(g, 4)
        ys = n % 2
        if part is None:
            s = g % 2
            XR, rXR, kXR, kST = xr[s], "xr%d" % s, "xrl%d" % s, "st%d" % s
            cs = list(range(16))
        else:
            XR, rXR, kXR, kST = tail_buf(tt)
            lo, hi = TAIL_BOUNDS[part], TAIL_BOUNDS[part + 1]
            cs = [c for c in range(16) if lo <= (c % 8) < hi]
        if part in (None, 0):
            dma("sp", XR[:], x_d[g * 128:(g + 1) * 128, :], [], [rXR], kXR, 524288)
        for half in range(2):
            obs = OPB if part is None else OPB_TAIL
            b = obs[opc[0] % len(obs)]
            opc[0] += 1

            def emit(h, b=b, half=half):
                last = None
                for c in cs:
                    last = h.matmul(bank[b], lhsT=yT[ys][:, c, tt * 128:(tt + 1) * 128],
                                    rhs=wout[:, c, half * 512:(half + 1) * 512], start=(c == cs[0]), stop=(c == cs[-1]))
                return last
            P.add("pe", emit, reads=["yT%d.%d" % (ys, c) for c in cs] + WOUTR, writes=[psres(b)],
                  dur=len(cs) * T_MM)
            P.add("dve", (lambda h, b=b, half=half: h.tensor_tensor(out=XR[:, half * 512:(half + 1) * 512],
                                                                    in0=bank[b], in1=XR[:, half * 512:(half + 1) * 512],
                                                                    op=ALU.add)),
                  reads=[psres(b), rXR], writes=[rXR], dur=T_DVE_PS)
        if part is not None:
            return
        out_norm(g, XR, rXR, kST)

    def tail_buf(tt):
        return [(xr[0], "xr0", "xrl0", "st0"), (xr[1], "xr1", "xrl1", "st1"),
                (xin[0], "xin0", "xin0", "st2"), (xin[1], "xin1", "xin1", "st3")][tt]

    def out_norm(g, XR, rXR, kST):
        s = g % 2
        P.add("act", lambda h: h.activation(out=junk[:], in_=XR[:], func=AF.Square, accum_out=st_o[s][:, 0:1]),
              reads=[rXR], writes=["junk", "sto%d" % s], dur=1.15)
        if g >= (NT - 1) * 4 and TAIL_OPT and NT >= 3:
            P.add("act", lambda h: h.activation(out=st_o[s][:, 1:2], in_=st_o[s][:, 0:1], func=AF.Ln, scale=1.0 / D,
                                                bias=epsb[:]), tbl="E",
                  reads=["sto%d" % s, "epsb"], writes=["sto%db" % s], dur=0.25)
            P.add("act", lambda h: h.activation(out=st_o[s][:, 2:3], in_=st_o[s][:, 1:2], func=AF.Exp, scale=-0.5), tbl="E",
                  reads=["sto%db" % s], writes=["sto%dc" % s], dur=0.25)
        else:
            P.add("dve", lambda h: h.tensor_scalar(out=st_o[s][:, 1:2], in0=st_o[s][:, 0:1], scalar1=1.0 / D,
                                                   scalar2=EPS, op0=ALU.mult, op1=ALU.add),
                  reads=["sto%d" % s], writes=["sto%db" % s], dur=T_TINY)
            P.add("pool", lambda h: h.tensor_tensor(out=st_o[s][:, 2:3], in0=st_o[s][:, 1:2], in1=mhalf[:], op=ALU.pow),
                  reads=["sto%db" % s, "mhalf"], writes=["sto%dc" % s], dur=0.5)
        P.add("dve", lambda h: h.scalar_tensor_tensor(out=XR[:], in0=XR[:], scalar=st_o[s][:, 2:3], in1=fgb[:],
                                                      op0=ALU.mult, op1=ALU.mult),
              reads=[rXR, "sto%dc" % s, "fgb"], writes=[rXR], dur=1.25)
        dma("sp", out_d[g * 128:(g + 1) * 128, :], XR[:], [rXR], [], kST, 524288, final=True)

    for tt in range(4):
        in_tile(tt)
    tap("xnT", xnT[0][:], ["xnT0.%d" % t for t in range(4)], [128, NK, TC], BF16)
    for n in range(NT):
        for j in range(8):
            wslot, parts = load_group(n, j)
            if n == WOUT_N and j < 4:
                load_wout(j)
            if n + 1 < NT and j in IN_J:
                in_tile((n + 1) * 4 + IN_J.index(j))
            if n == NT - 1 and TAIL_OPT and NT >= 3:
                lru_chunk(n, j, wslot, parts)
                conv_chunk(n, j, wslot, parts)
                for pi in range(len(TAIL_BOUNDS) - 2):
                    if j == min(7, TAIL_BOUNDS[pi + 1] - 1 + TAIL_LAG) and pi not in tail_done:
                        tail_done.append(pi)
                        for tt in range(4):
                            out_tile(n * 4 + tt, part=pi)
            else:
                conv_chunk(n, j, wslot, parts)
                lru_chunk(n, j, wslot, parts)
            if n == 1 and WOUT_N == 1:
                if j >= 4:
                    out_tile(j - 4)
            elif n == NT - 1 and TAIL_OPT and NT >= 3:
                if j < 4:
                    out_tile((n - 1) * 4 + j)
            elif n > 0 and j in OUT_J:
                out_tile((n - 1) * 4 + OUT_J.index(j))
    if TAIL_OPT and NT >= 3:
        last = len(TAIL_BOUNDS) - 2
        for pi in range(last + 1):
            if pi not in tail_done:
                for tt in range(4):
                    out_tile((NT - 1) * 4 + tt, part=pi)
        for tt in range(4):
            XR, rXR, kXR, kST = tail_buf(tt)
            out_norm((NT - 1) * 4 + tt, XR, rXR, kST)
    else:
        for tt in range(4):
            out_tile((NT - 1) * 4 + tt)

    P.schedule()
    handles = {"pe": nc.tensor, "act": nc.scalar, "dve": nc.vector, "pool": nc.gpsimd, "sp": nc.sync}
    P.emit(handles, sems, get_dma_sem)
    return nc, es, P


COL_BASE = (2048, 1024, 3072, 0, 4096, 5120)


def relayout_w_in(w):
    w4 = w.reshape(NK, 128, 6, 8, 128)
    order = [b // 1024 for b in COL_BASE]
    w5 = w4[:, :, order]
    return np.ascontiguousarray(w5.transpose(3, 1, 0, 2, 4).reshape(8, 128, NK * 768))


def host_inputs(inputs, S=4096):
    f32 = np.float32
    par = np.concatenate([
        np.asarray(inputs["conv_w"], f32).reshape(3, D),
        np.asarray(inputs["lru_conv_w"], f32).reshape(4, D),
        np.asarray(inputs["lru_conv_b"], f32).reshape(1, D),
        np.asarray(inputs["b_a"], f32).reshape(1, D),
        np.asarray(inputs["b_i"], f32).reshape(1, D),
        np.asarray(inputs["lam"], f32).reshape(1, D),
        np.asarray(inputs["conv_out_g"], f32).reshape(1, D),
        np.asarray(inputs["lru_out_g"], f32).reshape(1, D),
        np.zeros((1, D), f32),
    ], axis=0)
    blk = np.zeros((128, 128), f32)
    blk[:64, :64] = 1.0
    blk[64:, 64:] = 1.0
    shared = {
        "w_in_r": relayout_w_in(np.asarray(inputs["w_in"], f32)),
        "w_out_r": np.ascontiguousarray(np.asarray(inputs["w_out"], f32).reshape(16, 128, D).transpose(1, 0, 2)
                                        .reshape(128, 16 * D)),
        "w_a": np.ascontiguousarray(np.asarray(inputs["w_a"], f32)),
        "w_i": np.ascontiguousarray(np.asarray(inputs["w_i"], f32)),
        "par": np.ascontiguousarray(par),
        "ln_g": np.asarray(inputs["ln_g"], f32).reshape(1, D),
        "final_g": np.asarray(inputs["final_g"], f32).reshape(1, D),
        "ident_bf": np.eye(128, dtype=f32).astype(ml_dtypes.bfloat16),
        "ident_f": np.eye(128, dtype=f32),
        "ones_bf": np.ones((128, 128), f32).astype(ml_dtypes.bfloat16),
        "blk_bf": blk.astype(ml_dtypes.bfloat16),
    }
    return shared


def kernel(**inputs):
    x = np.asarray(inputs["x"], np.float32)
    B, S, _ = x.shape
    shared = host_inputs(inputs)
    nc, es, P = build_program(NT=S // TC)
    in_maps = []
    for b in range(B):
        m = dict(shared)
        m["x"] = np.ascontiguousarray(x[b])
        in_maps.append(m)
    res = run_bass_kernel_spmd(nc, in_maps, core_ids=list(range(B)))
    out = np.stack([np.asarray(r["out"], np.float32) for r in res.results], axis=0)
    return out
```

```python
import math
from contextlib import ExitStack

import numpy as np
import ml_dtypes
import concourse.bass as bass
import concourse.mybir as mybir
from concourse.bass_utils import run_bass_kernel_spmd

F32 = mybir.dt.float32
BF16 = mybir.dt.bfloat16
AF = mybir.ActivationFunctionType
ALU = mybir.AluOpType

ENGS = ("pe", "act", "dve", "pool", "sp")
TBL_PEN = 1.0
JITTER_SEED = 3
EVAC_PRIO = 1
EVAC_RES = set()


class Op:
    __slots__ = ("idx", "eng", "emit", "reads", "writes", "dur", "deps", "kind",
                 "dma_key", "lat", "name", "signals", "count", "fin", "start", "tbl", "prio")

    def __init__(self):
        self.deps = set()
        self.signals = False
        self.count = 0


class Prog:
    def __init__(self):
        self.ops = []
        self.last_w = {}
        self.readers = {}
        self.final_dma = []

    def add(self, eng, emit, reads=(), writes=(), dur=0.3, kind="c", dma_key=None,
            lat=0.0, name="", final=False, tbl=None):
        op = Op()
        op.tbl = tbl
        op.prio = 1
        op.idx = len(self.ops)
        op.eng, op.emit, op.dur, op.kind = eng, emit, dur, kind
        op.dma_key, op.lat, op.name = dma_key, lat, name
        op.reads, op.writes = tuple(reads), tuple(writes)
        if eng != "pe" and kind != "dma" and any(r in EVAC_RES for r in op.reads):
            op.prio = 0
        if kind == "dma":
            op.prio = 0
        for r in op.reads:
            w = self.last_w.get(r)
            if w is not None:
                op.deps.add(w)
        for r in op.writes:
            w = self.last_w.get(r)
            if w is not None:
                op.deps.add(w)
            for rd in self.readers.get(r, ()):
                op.deps.add(rd)
        for r in op.reads:
            self.readers.setdefault(r, []).append(op.idx)
        for r in op.writes:
            self.last_w[r] = op.idx
            self.readers[r] = []
        op.deps.discard(op.idx)
        self.ops.append(op)
        if final:
            self.final_dma.append(op.idx)
        return op

    def schedule(self, sync_lat=0.30):
        ops = self.ops
        n = len(ops)
        if JITTER_SEED:
            import random
            rng = random.Random(JITTER_SEED)
            for o in ops:
                o.dur *= 1.0 + 0.08 * (rng.random() - 0.5)
        succ = [[] for _ in range(n)]
        npred = [0] * n
        for o in ops:
            npred[o.idx] = len(o.deps)
            for d in o.deps:
                succ[d].append(o.idx)
        eng_free = {e: 0.0 for e in ENGS}
        avail = {e: {} for e in ENGS}
        dma_free = {e: 0.0 for e in ENGS}
        for o in ops:
            o.fin = None
            if npred[o.idx] == 0:
                avail[o.eng][o.idx] = 0.0
        order = {e: [] for e in ENGS}
        done = 0
        cur_tbl = None
        self.nswitch = 0
        while done < n:
            best = None
            for e in ENGS:
                a = avail[e]
                if not a:
                    continue
                ef = eng_free[e]
                if e == "act":
                    c = min((max(rt, ef) + (TBL_PEN if (ops[i].tbl is not None and ops[i].tbl != cur_tbl) else 0.0),
                             ops[i].prio * EVAC_PRIO, i) for i, rt in a.items())
                else:
                    c = min((max(rt, ef), ops[i].prio * EVAC_PRIO, i) for i, rt in a.items())
                if best is None or c < best[0]:
                    best = (c, e)
            (st, _pr, idx), e = best
            del avail[e][idx]
            o = ops[idx]
            if e == "act" and o.tbl is not None and o.tbl != cur_tbl:
                st = st - TBL_PEN + 1.3
                cur_tbl = o.tbl
                self.nswitch += 1
            o.start = st
            eng_free[e] = st + o.dur
            if o.kind == "dma":
                t0 = max(st + o.dur, dma_free[e])
                dma_free[e] = t0 + o.lat
                o.fin = t0 + o.lat + 2.0
            else:
                o.fin = st + o.dur
            order[e].append(idx)
            done += 1
            for s in succ[idx]:
                npred[s] -= 1
                if npred[s] == 0:
                    so = ops[s]
                    rt = 0.0
                    for d in so.deps:
                        rt = max(rt, ops[d].fin + sync_lat)
                    avail[so.eng][s] = rt
        self.order = order
        self.makespan = max(o.fin for o in ops)
        self.busy = {e: sum(ops[i].dur for i in order[e]) for e in ENGS}
        return order

    def emit(self, handles, sems, get_dma_sem):
        ops = self.ops
        order = self.order
        for o in ops:
            for d in o.deps:
                p = ops[d]
                if p.eng == "pe" and o.eng == "pe" and p.kind != "dma" and o.kind != "dma":
                    continue
                p.signals = True
        for i in self.final_dma:
            ops[i].signals = True
        cnt = {e: 0 for e in ENGS}
        dcnt = {}
        for e in ENGS:
            for idx in order[e]:
                o = ops[idx]
                if o.kind == "dma":
                    dcnt[o.dma_key] = dcnt.get(o.dma_key, 0) + 1
                    o.count = dcnt[o.dma_key] * 16
                    o.signals = True
                elif o.signals:
                    cnt[e] += 1
                    o.count = cnt[e]
        nwait = 0
        for e in ENGS:
            h = handles[e]
            waited = {}
            for idx in order[e]:
                o = ops[idx]
                need = {}
                for d in o.deps:
                    p = ops[d]
                    if p.kind == "dma":
                        key = ("d", p.dma_key)
                        sem = get_dma_sem(p.dma_key)
                    else:
                        if p.eng == "pe" and e == "pe" and o.kind != "dma":
                            continue
                        key = ("e", p.eng)
                        sem = sems[p.eng]
                    if need.get(key, (None, 0))[1] < p.count:
                        need[key] = (sem, p.count)
                for key, (sem, val) in need.items():
                    if waited.get(key, 0) < val:
                        h.wait_ge(sem, val)
                        waited[key] = val
                        nwait += 1
                inst = o.emit(h)
                if o.signals:
                    if o.kind == "dma":
                        inst.then_inc(get_dma_sem(o.dma_key), 16)
                    else:
                        inst.then_inc(sems[e], 1)
        h = handles["sp"]
        fin = {}
        for i in self.final_dma:
            o = ops[i]
            fin[o.dma_key] = max(fin.get(o.dma_key, 0), o.count)
        for k, v in fin.items():
            h.wait_ge(get_dma_sem(k), v)
        self.nwait = nwait


D = 1024
NK = 8
TC = 512
EPS = 1e-6
NPAR = 14
NCS = 2
NLS = 3
NWS = 3
NXIN = 3
SPLIT = (8, 8)
WOUT_N = 1
PJB = (0, 1, 2, 3, 6)
SQ_POOL = False
LATE_G = True
IN_J = (0, 1, 2, 3)
OUT_J = (3, 4, 5, 6)
TAIL_OPT = True
TAIL_BOUNDS = (0, 4, 6, 7, 8)
TAIL_LAG = 2
OPB_TAIL = (7, 6, 0, 1, 2, 3)
BANKS = {"gr": 4, "gi": 4, "cs": 4, "ls": 5}
OPB = (7,)
LN_HALF = math.log(0.5)

T_MM = 0.222
T_ACT = 0.6
T_ACT_P = 0.56
T_DVE_TT = 0.79
T_DVE_TS = 0.58
T_DVE_PS = 0.65
T_SCAN = 1.25
T_POOL_TT = 1.4
T_POOL_CP = 0.75
T_TINY = 0.15
T_PTINY = 0.44


def build_program(NT=8, taps=False):
    S = NT * TC
    NTT = S // 128
    nc = bass.Bass("TRN2", target_bir_lowering=False)
    dr = lambda name, shape, dt, kind="ExternalInput": nc.dram_tensor(name, shape, dt, kind=kind).ap()
    x_d = dr("x", [S, D], F32)
    win_d = dr("w_in_r", [8, 128, NK * 768], F32)
    wout_d = dr("w_out_r", [128, 16 * D], F32)
    wa_d = dr("w_a", [16, 64, 64], F32)
    wi_d = dr("w_i", [16, 64, 64], F32)
    par_d = dr("par", [NPAR, D], F32)
    lng_d = dr("ln_g", [1, D], F32)
    fg_d = dr("final_g", [1, D], F32)
    idb_d = dr("ident_bf", [128, 128], BF16)
    idf_d = dr("ident_f", [128, 128], F32)
    ones_d = dr("ones_bf", [128, 128], BF16)
    blk_d = dr("blk_bf", [128, 128], BF16)
    out_d = dr("out", [S, D], F32, kind="ExternalOutput")
    wsc_d = dr("wsc", [8, 128, NK * 768], BF16, kind="Internal")

    es = ExitStack()
    sb = lambda name, shape, dt: es.enter_context(nc.sbuf_tensor(name, shape, dt))
    wbuf = [sb("wbuf%d" % i, [128, NK, 768], BF16) for i in range(NWS)]
    wout = sb("wout", [128, 16, D], BF16)
    gw = sb("gw", [128, 2, 8, 128], BF16)
    idb = sb("idb", [128, 128], BF16)
    idf = sb("idf", [128, 128], F32)
    ones = sb("ones", [128, 128], BF16)
    blk = sb("blk", [128, 128], BF16)
    lng = sb("lng", [128, D], F32)
    fgb = sb("fgb", [128, D], F32)
    PAR = sb("PAR", [128, 8, 16], F32)
    xin = [sb("xin%d" % i, [128, D], F32) for i in range(NXIN)]
    xr = [sb("xr%d" % i, [128, D], F32) for i in range(2)]
    xn = [sb("xn%d" % i, [128, D], BF16) for i in range(2)]
    junk = sb("junk", [128, D], BF16)
    st_i = [sb("sti%d" % i, [128, 4], F32) for i in range(2)]
    st_o = [sb("sto%d" % i, [128, 4], F32) for i in range(2)]
    mhalf = sb("mhalf", [128, 1], F32)
    xnT = [sb("xnT%d" % i, [128, NK, TC], BF16) for i in range(2)]
    yT = [sb("yT%d" % i, [128, 16, TC], BF16) for i in range(2)]
    cA = [sb("cA%d" % i, [128, TC + 2], F32) for i in range(NCS)]
    cB = [sb("cB%d" % i, [128, TC], F32) for i in range(NCS)]
    cSG = [sb("cSG%d" % i, [128, TC], F32) for i in range(NCS)]
    cY2 = [sb("cY2%d" % i, [128, TC], BF16) for i in range(NCS)]
    l1 = [sb("l1_%d" % i, [128, TC + 3], F32) for i in range(NLS)]
    l2 = [sb("l2_%d" % i, [128, TC], F32) for i in range(NLS)]
    l3 = [sb("l3_%d" % i, [128, TC], F32) for i in range(NLS)]
    l4 = [sb("l4_%d" % i, [128, TC], F32) for i in range(NLS)]
    l5 = [sb("l5_%d" % i, [128, TC], F32) for i in range(NLS)]
    lUB = [sb("lUB%d" % i, [128, TC], BF16) for i in range(NLS)]
    lH2 = [sb("lH2%d" % i, [128, TC], BF16) for i in range(NLS)]
    HC = sb("HC", [128, 8, 2], F32)
    HL = sb("HL", [128, 8, 4], F32)
    HS = sb("HS", [128, 8], F32)
    ps = es.enter_context(nc.psum_tensor("ps", [128, 7, TC], F32))
    psT = es.enter_context(nc.psum_tensor("psT", [128, 2 * TC], BF16))
    bank = [ps[:, b, :] for b in range(7)] + [psT[:].bitcast(F32)]

    sems = {e: es.enter_context(nc.semaphore("s_" + e)) for e in ("pe", "act", "dve", "pool")}
    dsems = {}

    def get_dma_sem(k):
        if k not in dsems:
            dsems[k] = es.enter_context(nc.semaphore("d_" + k))
        return dsems[k]

    P = Prog()
    tapn = [0]
    psres = lambda b: "psT" if b == 7 else "ps%d" % b

    def tap(name, ap, res, shape, dt=F32, cond=True):
        if not (taps and cond):
            return
        t = nc.dram_tensor("tap_" + name, list(shape), dt, kind="ExternalOutput").ap()
        tapn[0] += 1
        P.add("sp", (lambda h: h.dma_start(out=t, in_=ap)), reads=list(res), writes=[], kind="dma",
              dma_key="tap%d" % tapn[0], lat=1.0, dur=0.1, final=True)

    def dma(q, out, in_, reads, writes, key, nbytes, final=False, name="", **kw):
        return P.add(q, (lambda h: h.dma_start(out=out, in_=in_, **kw)), reads=reads, writes=writes,
                     kind="dma", dma_key=key, lat=nbytes / 220e3, dur=(1.5 if q == "pool" else 0.1),
                     final=final, name=name)

    epsb = sb("epsb", [128, 1], F32)
    oneb = sb("oneb", [128, 1], F32)
    lnhb = sb("lnhb", [128, 1], F32)
    P.add("pool", lambda h: h.memset(epsb[:], EPS), writes=["epsb"], dur=T_TINY)
    P.add("pool", lambda h: h.memset(oneb[:], 1.0), writes=["oneb"], dur=T_TINY)
    P.add("pool", lambda h: h.memset(lnhb[:], LN_HALF), writes=["lnhb"], dur=T_TINY)

    bc = lambda t: bass.AP(t.tensor, 0, [[0, 128], [1, D]])
    dma("sp", idb[:], idb_d, [], ["idb"], "c0", 32768)
    dma("sp", idf[:], idf_d, [], ["idf"], "c1", 65536)
    dma("sp", xr[0][0:NPAR, :], par_d, [], ["xr0"], "c4", NPAR * 4096)
    for g0, (XB0, rX0, kX0) in enumerate([(xin[0], "xin0", "xin0"), (xin[1], "xin1", "xin1"), (xin[2], "xin2", "xin2"),
                                          (xr[1], "xr1", "xrl1")]):
        dma("sp", XB0[:], x_d[g0 * 128:(g0 + 1) * 128, :], [], [rX0], kX0, 524288)
        if g0 == 0:
            dma("sp", lng[:], bc(lng_d), [], ["lng"], "c5", 524288)

    dma("sp", ones[:], ones_d, [], ["ones"], "c2", 32768)
    dma("sp", blk[:], blk_d, [], ["blk"], "c3", 32768)
    dma("sp", fgb[:], bc(fg_d), [], ["fgb"], "c6", 524288)
    P.add("pool", lambda h: h.memset(mhalf[:], -0.5), writes=["mhalf"], dur=T_TINY)
    P.add("pool", lambda h: h.memset(HC[:], 0.0), writes=["HC%d" % j for j in range(8)], dur=T_TINY)
    P.add("pool", lambda h: h.memset(HL[:], 0.0), writes=["HL%d" % j for j in range(8)], dur=T_TINY)
    P.add("pool", lambda h: h.memset(HS[:], 0.0), writes=["HS%d" % j for j in range(8)], dur=T_TINY)
    P.add("pool", lambda h: h.memset(gw[:], 0.0), writes=["gw"], dur=1.0)

    def emit_par_T(h):
        last = None
        for j in range(8):
            last = h.transpose(out=bank[4][:, j * 16:j * 16 + NPAR], in_=xr[0][0:NPAR, j * 128:(j + 1) * 128],
                               identity=idf[0:NPAR, 0:NPAR])
        return last
    P.add("pe", emit_par_T, reads=["xr0", "idf"], writes=["ps4"], dur=1.0)
    P.add("dve", lambda h: h.tensor_copy(out=PAR[:, :, 0:NPAR],
                                         in_=bank[4][:, 0:128].rearrange("p (j c) -> p j c", c=16)[:, :, 0:NPAR]),
          reads=["ps4"], writes=["PAR"], dur=0.3)
    P.add("act", lambda h: h.activation(out=PAR[:, :, 13], in_=PAR[:, :, 10], func=AF.Exp, scale=-1.0), tbl="E",
          reads=["PAR"], writes=["PARd"], dur=0.3)
    P.add("act", lambda h: h.activation(out=PAR[:, :, 13], in_=PAR[:, :, 13], func=AF.Ln, bias=oneb[:]), tbl="E",
          reads=["PARd", "oneb"], writes=["PARd"], dur=0.3)
    P.add("dve", lambda h: h.tensor_scalar(out=PAR[:, :, 13], in0=PAR[:, :, 13], scalar1=-4.0, scalar2=None,
                                           op0=ALU.mult), reads=["PARd"], writes=["PARd"], dur=0.2)
    P.add("dve", lambda h: h.tensor_scalar(out=PAR[:, :, 14:16], in0=PAR[:, :, 8:10], scalar1=0.5, scalar2=None,
                                           op0=ALU.mult), reads=["PAR"], writes=["PARe"], dur=0.2)
    PARR = ["PAR", "PARd", "PARe"]
    tap("PAR", PAR[:], PARR, [128, 8, 16])

    for gi, wd in enumerate((wa_d, wi_d)):
        src = wd.rearrange("(c two) d e -> two d c e", two=2)
        for two in range(2):
            dma("pool", gw[two * 64:(two + 1) * 64, gi, :, two * 64:(two + 1) * 64], src[two],
                ["gw"], ["gw%d%d" % (gi, two)], "gw%d%d" % (gi, two), 131072)
    GWR = ["gw%d%d" % (a, b) for a in range(2) for b in range(2)]

    wcount = [0]
    late_ok = [False]

    def load_group(n, j):
        slot = wcount[0] % NWS
        wcount[0] += 1
        parts = ["wb%d.%d" % (slot, s) for s in range(6)]
        ready_n = 1 if j < SPLIT[0] else (2 if j < SPLIT[1] else 3)
        cast_load = n < ready_n
        if cast_load:
            hold = ["xr1"] if (n == 0 and j in (1, 2)) else (["xin1"] if (n == 0 and j == 0) else [])
            dma("pool", wbuf[slot][:].rearrange("p k e -> p (k e)").rearrange("p (a b) -> p a b", b=2048),
                win_d[j].rearrange("p (a b) -> p a b", b=2048), hold, parts, "wb%d" % slot, 128 * 6144 * 4)
            if NT > n + 1 and n == ready_n - 1:
                dma("sp", wsc_d[j], wbuf[slot][:].rearrange("p k e -> p (k e)"), parts, ["wsc%d" % j],
                    "ws%d" % slot, 1572864)
        else:
            dma("sp", wbuf[slot][:].rearrange("p k e -> p (k e)"), wsc_d[j], ["wsc%d" % j], parts,
                "wl%d" % slot, 1572864)
        late_ok[0] = not cast_load
        return slot, parts

    def load_wout(q):
        src = wout_d.rearrange("p (c d) -> p c d", d=D)
        for hh in range(2):
            dma("pool", wout[:, q * 4 + 2 * hh:q * 4 + 2 * hh + 2, :], src[:, q * 4 + 2 * hh:q * 4 + 2 * hh + 2, :],
                ["wsc7"], ["wout%d.%d" % (q, hh)], "wo%d%d" % (q, hh), 1048576)
    WOUTR = ["wout%d.%d" % (q, hh) for q in range(4) for hh in range(2)]

    def in_tile(g):
        n, tt = divmod(g, 4)
        ns = n % 2
        if g == 3:
            XB, rX, kX = xr[1], "xr1", "xrl1"
        else:
            s = (g if g < 3 else g - 1) % NXIN
            XB, rX, kX = xin[s], "xin%d" % s, "xin%d" % s
        if g >= 4:
            dma("sp", XB[:], x_d[g * 128:(g + 1) * 128, :], [], [rX], kX, 524288)
        s2 = g % 2
        P.add("act", lambda h: h.activation(out=junk[:], in_=XB[:], func=AF.Square, accum_out=st_i[s2][:, 0:1]),
              reads=[rX], writes=["junk", "sti%d" % s2], dur=1.15)
        if g < 4:
            P.add("act", lambda h: h.activation(out=st_i[s2][:, 1:2], in_=st_i[s2][:, 0:1], func=AF.Ln, scale=1.0 / D,
                                                bias=epsb[:]), tbl="E",
                  reads=["sti%d" % s2, "epsb"], writes=["sti%db" % s2], dur=0.25)
            P.add("act", lambda h: h.activation(out=st_i[s2][:, 2:3], in_=st_i[s2][:, 1:2], func=AF.Exp, scale=-0.5), tbl="E",
                  reads=["sti%db" % s2], writes=["sti%dc" % s2], dur=0.25)
        else:
            P.add("dve", lambda h: h.tensor_scalar(out=st_i[s2][:, 1:2], in0=st_i[s2][:, 0:1], scalar1=1.0 / D,
                                                   scalar2=EPS, op0=ALU.mult, op1=ALU.add),
                  reads=["sti%d" % s2], writes=["sti%db" % s2], dur=T_TINY)
            P.add("pool", lambda h: h.tensor_tensor(out=st_i[s2][:, 2:3], in0=st_i[s2][:, 1:2], in1=mhalf[:], op=ALU.pow),
                  reads=["sti%db" % s2, "mhalf"], writes=["sti%dc" % s2], dur=0.5)
        P.add("dve", lambda h: h.scalar_tensor_tensor(out=xn[s2][:], in0=XB[:], scalar=st_i[s2][:, 2:3], in1=lng[:],
                                                      op0=ALU.mult, op1=ALU.mult),
              reads=[rX, "sti%dc" % s2, "lng"], writes=["xn%d" % s2], dur=1.25)
        def emit_T(h):
            last = None
            for k in range(NK):
                last = h.transpose(out=psT[:, k * 128:(k + 1) * 128], in_=xn[s2][:, k * 128:(k + 1) * 128],
                                   identity=idb[:])
            return last
        P.add("pe", emit_T, reads=["xn%d" % s2, "idb"], writes=["psT"], dur=0.9)
        P.add("dve", lambda h: h.tensor_copy(out=xnT[ns][:, :, tt * 128:(tt + 1) * 128],
                                             in_=psT[:].rearrange("p (k t) -> p k t", t=128)),
              reads=["psT"], writes=["xnT%d.%d" % (ns, tt)], dur=0.75)

    axc = [0]
    tail_done = []
    pjc = [0]
    ccnt = [0]
    lcnt = [0]

    def pj_bank():
        b = PJB[pjc[0] % len(PJB)]
        pjc[0] += 1
        EVAC_RES.add("ps%d" % b)
        return b

    def ax_bank(kind):
        return BANKS[kind]

    def proj(n, wslot, wres, col0, b):
        ns = n % 2

        def emit(h):
            last = None
            for k in range(NK):
                last = h.matmul(bank[b], lhsT=wbuf[wslot][:, k, col0:col0 + 128], rhs=xnT[ns][:, k, :],
                                start=(k == 0), stop=(k == NK - 1))
            return last
        P.add("pe", emit, reads=[wres] + ["xnT%d.%d" % (ns, t) for t in range(4)], writes=["ps%d" % b],
              dur=NK * T_MM)

    def conv_chunk(n, j, wslot, parts):
        sc = ccnt[0] % NCS
        ccnt[0] += 1
        ys = n % 2
        A, Bt, SG, Y2 = cA[sc], cB[sc], cSG[sc], cY2[sc]
        rA, rAh, rB, rSG, rY2 = "cA%d" % sc, "cAh%d" % sc, "cB%d" % sc, "cSG%d" % sc, "cY2%d" % sc
        pw = lambda c: PAR[:, j, c:c + 1]
        b_xc = pj_bank()
        proj(n, wslot, parts[0], 0, b_xc)
        P.add("act", lambda h: h.activation(out=Bt[:], in_=bank[b_xc], func=AF.Copy),
              reads=["ps%d" % b_xc], writes=[rB], dur=T_ACT_P)
        T0 = (n == 0 and j == 0)
        tap("c_xc", Bt[:], [rB], [128, TC], cond=T0)
        P.add("pool", lambda h: h.tensor_copy(out=A[:, 0:2], in_=HC[:, j, :]), reads=["HC%d" % j], writes=[rAh],
              dur=T_PTINY)
        b_c = pj_bank()
        proj(n, wslot, parts[1], 128, b_c)
        P.add("dve", lambda h: h.tensor_tensor(out=A[:, 2:TC + 2], in0=bank[b_c], in1=Bt[:], op=ALU.mult),
              reads=["ps%d" % b_c, rB], writes=[rA], dur=T_DVE_PS)
        P.add("pool", lambda h: h.tensor_copy(out=HC[:, j, :], in_=A[:, TC:TC + 2]), reads=[rA], writes=["HC%d" % j],
              dur=T_PTINY)
        tap("c_cx", A[:], [rA, rAh], [128, TC + 2], cond=T0)
        def do_g():
            b_g = pj_bank()
            proj(n, wslot, parts[2], 256, b_g)
            P.add("act", lambda h: h.activation(out=SG[:], in_=bank[b_g], func=AF.Silu), tbl="B",
                  reads=["ps%d" % b_g], writes=[rSG], dur=T_ACT_P)
            tap("c_sg", SG[:], [rSG], [128, TC], cond=T0)
        LG = LATE_G and late_ok[0]
        if not LG:
            do_g()
        P.add("dve", lambda h: h.tensor_scalar(out=Bt[:], in0=A[:, 2:TC + 2], scalar1=pw(2), scalar2=None, op0=ALU.mult),
              reads=[rA] + PARR, writes=[rB], dur=T_DVE_TS)
        P.add("dve", lambda h: h.scalar_tensor_tensor(out=Bt[:], in0=A[:, 1:TC + 1], scalar=pw(1), in1=Bt[:],
                                                      op0=ALU.mult, op1=ALU.add),
              reads=[rA, rAh, rB] + PARR, writes=[rB], dur=T_DVE_TT)
        P.add("dve", lambda h: h.scalar_tensor_tensor(out=Bt[:], in0=A[:, 0:TC], scalar=pw(0), in1=Bt[:],
                                                      op0=ALU.mult, op1=ALU.add),
              reads=[rA, rAh, rB] + PARR, writes=[rB], dur=T_DVE_TT)
        tap("c_cv", Bt[:], [rB], [128, TC], cond=T0)
        b_b = pj_bank()
        proj(n, wslot, parts[3], 384, b_b)
        P.add("dve", lambda h: h.tensor_tensor(out=Bt[:], in0=bank[b_b], in1=Bt[:], op=ALU.mult),
              reads=["ps%d" % b_b, rB], writes=[rB], dur=T_DVE_PS)
        tap("c_y", Bt[:], [rB], [128, TC], cond=T0)
        if LG:
            do_g()
        if SQ_POOL:
            P.add("pool", lambda h: h.tensor_tensor(out=Y2[:], in0=Bt[:], in1=Bt[:], op=ALU.mult), reads=[rB], writes=[rY2],
                  dur=T_POOL_TT)
        else:
            P.add("act", lambda h: h.activation(out=Y2[:], in_=Bt[:], func=AF.Square), reads=[rB], writes=[rY2], dur=T_ACT)
        ax = ax_bank("cs")
        P.add("pe", lambda h: h.matmul(bank[ax], lhsT=ones[:], rhs=Y2[:], start=True, stop=True),
              reads=[rY2, "ones"], writes=[psres(ax)], dur=T_MM)
        P.add("act", lambda h: h.activation(out=A[:, 0:TC], in_=bank[ax], func=AF.Ln, scale=1.0 / 128, bias=epsb[:]), tbl="E",
              reads=[psres(ax), "epsb"], writes=[rA, rAh], dur=T_ACT_P)
        P.add("act", lambda h: h.activation(out=A[:, 0:TC], in_=A[:, 0:TC], func=AF.Exp, scale=-0.5), tbl="E",
              reads=[rA, rAh], writes=[rA, rAh], dur=T_ACT)
        tap("c_rstd", A[:, 0:TC], [rA, rAh], [128, TC], cond=T0)
        P.add("dve", lambda h: h.scalar_tensor_tensor(out=Bt[:], in0=Bt[:], scalar=pw(11), in1=A[:, 0:TC],
                                                      op0=ALU.mult, op1=ALU.mult),
              reads=[rB, rA, rAh] + PARR, writes=[rB], dur=T_DVE_TT)
        P.add("pool", lambda h: h.tensor_tensor(out=yT[ys][:, j, :], in0=Bt[:], in1=SG[:], op=ALU.mult),
              reads=[rB, rSG], writes=["yT%d.%d" % (ys, j)], dur=T_POOL_TT)
        tap("c_yT", yT[ys][:, j, :], ["yT%d.%d" % (ys, j)], [128, TC], BF16, cond=T0)

    def lru_chunk(n, j, wslot, parts):
        sl = lcnt[0] % NLS
        lcnt[0] += 1
        ys = n % 2
        T1, T2, T3, T4, T5, UB, H2 = l1[sl], l2[sl], l3[sl], l4[sl], l5[sl], lUB[sl], lH2[sl]
        r1, r1h, r2, r3, r4, r5, rUB, rH2 = ["l%s_%d" % (t, sl) for t in ("1", "1h", "2", "3", "4", "5", "UB", "H2")]
        pw = lambda c: PAR[:, j, c:c + 1]
        b_x = pj_bank()
        proj(n, wslot, parts[4], 512, b_x)
        P.add("act", lambda h: h.activation(out=T1[:, 3:TC + 3], in_=bank[b_x], func=AF.Copy),
              reads=["ps%d" % b_x], writes=[r1], dur=T_ACT_P)
        P.add("pool", lambda h: h.tensor_copy(out=T1[:, 0:3], in_=HL[:, j, 0:3]), reads=["HL%d" % j], writes=[r1h],
              dur=T_PTINY)
        P.add("pool", lambda h: h.tensor_copy(out=HL[:, j, 0:3], in_=T1[:, TC:TC + 3]), reads=[r1], writes=["HL%d" % j],
              dur=T_PTINY)
        def do_gl():
            b_g = pj_bank()
            proj(n, wslot, parts[5], 640, b_g)
            P.add("act", lambda h: h.activation(out=T5[:], in_=bank[b_g], func=AF.Silu), tbl="B",
                  reads=["ps%d" % b_g], writes=[r5], dur=T_ACT_P)
        LG = LATE_G and late_ok[0]
        if not LG:
            do_gl()
        P.add("dve", lambda h: h.tensor_scalar(out=T2[:], in0=T1[:, 3:TC + 3], scalar1=pw(6), scalar2=pw(7),
                                               op0=ALU.mult, op1=ALU.add),
              reads=[r1] + PARR, writes=[r2], dur=T_DVE_TS)
        for k, eng in ((1, "dve"), (2, "dve"), (3, "dve")):
            P.add(eng, (lambda h, k=k: h.scalar_tensor_tensor(out=T2[:], in0=T1[:, 3 - k:TC + 3 - k], scalar=pw(6 - k),
                                                              in1=T2[:], op0=ALU.mult, op1=ALU.add)),
                  reads=[r1, r1h, r2] + PARR, writes=[r2], dur=(T_DVE_TT if eng == "dve" else T_POOL_TT))
        T0 = (n == 0 and j == 0)
        tap("l_xl", T1[:], [r1, r1h], [128, TC + 3], cond=T0)
        tap("l_u", T2[:], [r2], [128, TC], cond=T0)
        P.add("dve", lambda h: h.tensor_copy(out=UB[:], in_=T2[:]), reads=[r2], writes=[rUB], dur=T_DVE_TS)
        ax_r = ax_bank("gr")
        P.add("pe", lambda h: h.matmul(bank[ax_r], lhsT=gw[:, 0, j, :], rhs=UB[:], start=True, stop=True),
              reads=[rUB] + GWR, writes=[psres(ax_r)], dur=T_MM)
        P.add("act", lambda h: h.activation(out=T3[:], in_=bank[ax_r], func=AF.Tanh, scale=0.5, bias=pw(14)), tbl="B",
              reads=[psres(ax_r)] + PARR, writes=[r3], dur=T_ACT_P + 0.1)
        ax_i = ax_bank("gi")
        P.add("pe", lambda h: h.matmul(bank[ax_i], lhsT=gw[:, 1, j, :], rhs=UB[:], start=True, stop=True),
              reads=[rUB] + GWR, writes=[psres(ax_i)], dur=T_MM)
        P.add("act", lambda h: h.activation(out=T4[:], in_=bank[ax_i], func=AF.Tanh, scale=0.5, bias=pw(15)), tbl="B",
              reads=[psres(ax_i)] + PARR, writes=[r4], dur=T_ACT_P + 0.1)
        if LG:
            do_gl()
        tap("l_tr", T3[:], [r3], [128, TC], cond=T0)
        tap("l_ti", T4[:], [r4], [128, TC], cond=T0)
        P.add("act", lambda h: h.activation(out=T3[:], in_=T3[:], func=AF.Exp, scale=pw(13), bias=pw(13)), tbl="E",
              reads=[r3] + PARR, writes=[r3], dur=T_ACT + 0.2)
        P.add("pool", lambda h: h.tensor_tensor(out=T1[:, 0:TC], in0=T3[:], in1=T3[:], op=ALU.mult),
              reads=[r3], writes=[r1, r1h], dur=T_POOL_TT)
        P.add("act", lambda h: h.activation(out=T1[:, 0:TC], in_=T1[:, 0:TC], func=AF.Ln, scale=-1.0, bias=oneb[:]), tbl="E",
              reads=[r1, r1h, "oneb"], writes=[r1, r1h], dur=T_ACT)
        P.add("act", lambda h: h.activation(out=T1[:, 0:TC], in_=T1[:, 0:TC], func=AF.Exp, scale=0.5, bias=lnhb[:]), tbl="E",
              reads=[r1, r1h, "lnhb"], writes=[r1, r1h], dur=T_ACT)
        tap("l_a", T3[:], [r3], [128, TC], cond=T0)
        tap("l_mh", T1[:, 0:TC], [r1, r1h], [128, TC], cond=T0)
        P.add("dve", lambda h: h.scalar_tensor_tensor(out=T4[:], in0=T4[:], scalar=1.0, in1=T2[:],
                                                      op0=ALU.add, op1=ALU.mult),
              reads=[r4, r2], writes=[r4], dur=T_DVE_TT)
        P.add("pool", lambda h: h.tensor_tensor(out=T4[:], in0=T4[:], in1=T1[:, 0:TC], op=ALU.mult),
              reads=[r4, r1, r1h], writes=[r4], dur=T_POOL_TT)
        tap("l_drive", T4[:], [r4], [128, TC], cond=T0)
        P.add("dve", lambda h: h.tensor_tensor_scan(out=T2[:], data0=T3[:], data1=T4[:], initial=HS[:, j:j + 1],
                                                    op0=ALU.mult, op1=ALU.add),
              reads=[r3, r4, "HS%d" % j], writes=[r2], dur=T_SCAN)
        P.add("pool", lambda h: h.tensor_copy(out=HS[:, j:j + 1], in_=T2[:, TC - 1:TC]), reads=[r2],
              writes=["HS%d" % j], dur=T_PTINY)
        tap("l_h", T2[:], [r2], [128, TC], cond=T0)
        if SQ_POOL:
            P.add("pool", lambda h: h.tensor_tensor(out=H2[:], in0=T2[:], in1=T2[:], op=ALU.mult), reads=[r2], writes=[rH2],
                  dur=T_POOL_TT)
        else:
            P.add("act", lambda h: h.activation(out=H2[:], in_=T2[:], func=AF.Square), reads=[r2], writes=[rH2], dur=T_ACT)
        ax = ax_bank("ls")
        P.add("pe", lambda h: h.matmul(bank[ax], lhsT=blk[:], rhs=H2[:], start=True, stop=True),
              reads=[rH2, "blk"], writes=[psres(ax)], dur=T_MM)
        P.add("act", lambda h: h.activation(out=T4[:], in_=bank[ax], func=AF.Ln, scale=1.0 / 64, bias=epsb[:]), tbl="E",
              reads=[psres(ax), "epsb"], writes=[r4], dur=T_ACT_P)
        P.add("act", lambda h: h.activation(out=T4[:], in_=T4[:], func=AF.Exp, scale=-0.5), tbl="E",
              reads=[r4], writes=[r4], dur=T_ACT)
        P.add("dve", lambda h: h.scalar_tensor_tensor(out=T2[:], in0=T2[:], scalar=pw(12), in1=T4[:],
                                                      op0=ALU.mult, op1=ALU.mult),
              reads=[r2, r4] + PARR, writes=[r2], dur=T_DVE_TT)
        P.add("pool", lambda h: h.tensor_tensor(out=yT[ys][:, 8 + j, :], in0=T2[:], in1=T5[:], op=ALU.mult),
              reads=[r2, r5], writes=["yT%d.%d" % (ys, 8 + j)], dur=T_POOL_TT)
        tap("l_yT", yT[ys][:, 8 + j, :], ["yT%d.%d" % (ys, 8 + j)], [128, TC], BF16, cond=T0)

    opc = [0]

    def out_tile(g, part=None):
        n, tt = divmod(g, 4)
        ys = n % 2
        if part is None:
            s = g % 2
            XR, rXR, kXR, kST = xr[s], "xr%d" % s, "xrl%d" % s, "st%d" % s
            cs = list(range(16))
        else:
            XR, rXR, kXR, kST = tail_buf(tt)
            lo, hi = TAIL_BOUNDS[part], TAIL_BOUNDS[part + 1]
            cs = [c for c in range(16) if lo <= (c % 8) < hi]
        if part in (None, 0):
            dma("sp", XR[:], x_d[g * 128:(g + 1) * 128, :], [], [rXR], kXR, 524288)
        for half in range(2):
            obs = OPB if part is None else OPB_TAIL
            b = obs[opc[0] % len(obs)]
            opc[0] += 1

            def emit(h, b=b, half=half):
                last = None
                for c in cs:
                    last = h.matmul(bank[b], lhsT=yT[ys][:, c, tt * 128:(tt + 1) * 128],
                                    rhs=wout[:, c, half * 512:(half + 1) * 512], start=(c == cs[0]), stop=(c == cs[-1]))
                return last
            P.add("pe", emit, reads=["yT%d.%d" % (ys, c) for c in cs] + WOUTR, writes=[psres(b)],
                  dur=len(cs) * T_MM)
            P.add("dve", (lambda h, b=b, half=half: h.tensor_tensor(out=XR[:, half * 512:(half + 1) * 512],
                                                                    in0=bank[b], in1=XR[:, half * 512:(half + 1) * 512],
                                                                    op=ALU.add)),
                  reads=[psres(b), rXR], writes=[rXR], dur=T_DVE_PS)
        if part is not None:
            return
        out_norm(g, XR, rXR, kST)

    def tail_buf(tt):
        return [(xr[0], "xr0", "xrl0", "st0"), (xr[1], "xr1", "xrl1", "st1"),
                (xin[0], "xin0", "xin0", "st2"), (xin[1], "xin1", "xin1", "st3")][tt]

    def out_norm(g, XR, rXR, kST):
        s = g % 2
        P.add("act", lambda h: h.activation(out=junk[:], in_=XR[:], func=AF.Square, accum_out=st_o[s][:, 0:1]),
              reads=[rXR], writes=["junk", "sto%d" % s], dur=1.15)
        if g >= (NT - 1) * 4 and TAIL_OPT and NT >= 3:
            P.add("act", lambda h: h.activation(out=st_o[s][:, 1:2], in_=st_o[s][:, 0:1], func=AF.Ln, scale=1.0 / D,
                                                bias=epsb[:]), tbl="E",
                  reads=["sto%d" % s, "epsb"], writes=["sto%db" % s], dur=0.25)
            P.add("act", lambda h: h.activation(out=st_o[s][:, 2:3], in_=st_o[s][:, 1:2], func=AF.Exp, scale=-0.5), tbl="E",
                  reads=["sto%db" % s], writes=["sto%dc" % s], dur=0.25)
        else:
            P.add("dve", lambda h: h.tensor_scalar(out=st_o[s][:, 1:2], in0=st_o[s][:, 0:1], scalar1=1.0 / D,
                                                   scalar2=EPS, op0=ALU.mult, op1=ALU.add),
                  reads=["sto%d" % s], writes=["sto%db" % s], dur=T_TINY)
            P.add("pool", lambda h: h.tensor_tensor(out=st_o[s][:, 2:3], in0=st_o[s][:, 1:2], in1=mhalf[:], op=ALU.pow),
                  reads=["sto%db" % s, "mhalf"], writes=["sto%dc" % s], dur=0.5)
        P.add("dve", lambda h: h.scalar_tensor_tensor(out=XR[:], in0=XR[:], scalar=st_o[s][:, 2:3], in1=fgb[:],
                                                      op0=ALU.mult, op1=ALU.mult),
              reads=[rXR, "sto%dc" % s, "fgb"], writes=[rXR], dur=1.25)
        dma("sp", out_d[g * 128:(g + 1) * 128, :], XR[:], [rXR], [], kST, 524288, final=True)

    for tt in range(4):
        in_tile(tt)
    tap("xnT", xnT[0][:], ["xnT0.%d" % t for t in range(4)], [128, NK, TC], BF16)
    for n in range(NT):
        for j in range(8):
            wslot, parts = load_group(n, j)
            if n == WOUT_N and j < 4:
                load_wout(j)
            if n + 1 < NT and j in IN_J:
                in_tile((n + 1) * 4 + IN_J.index(j))
            if n == NT - 1 and TAIL_OPT and NT >= 3:
                lru_chunk(n, j, wslot, parts)
                conv_chunk(n, j, wslot, parts)
                for pi in range(len(TAIL_BOUNDS) - 2):
                    if j == min(7, TAIL_BOUNDS[pi + 1] - 1 + TAIL_LAG) and pi not in tail_done:
                        tail_done.append(pi)
                        for tt in range(4):
                            out_tile(n * 4 + tt, part=pi)
            else:
                conv_chunk(n, j, wslot, parts)
                lru_chunk(n, j, wslot, parts)
            if n == 1 and WOUT_N == 1:
                if j >= 4:
                    out_tile(j - 4)
            elif n == NT - 1 and TAIL_OPT and NT >= 3:
                if j < 4:
                    out_tile((n - 1) * 4 + j)
            elif n > 0 and j in OUT_J:
                out_tile((n - 1) * 4 + OUT_J.index(j))
    if TAIL_OPT and NT >= 3:
        last = len(TAIL_BOUNDS) - 2
        for pi in range(last + 1):
            if pi not in tail_done:
                for tt in range(4):
                    out_tile((NT - 1) * 4 + tt, part=pi)
        for tt in range(4):
            XR, rXR, kXR, kST = tail_buf(tt)
            out_norm((NT - 1) * 4 + tt, XR, rXR, kST)
    else:
        for tt in range(4):
            out_tile((NT - 1) * 4 + tt)

    P.schedule()
    handles = {"pe": nc.tensor, "act": nc.scalar, "dve": nc.vector, "pool": nc.gpsimd, "sp": nc.sync}
    P.emit(handles, sems, get_dma_sem)
    return nc, es, P


COL_BASE = (2048, 1024, 3072, 0, 4096, 5120)


def relayout_w_in(w):
    w4 = w.reshape(NK, 128, 6, 8, 128)
    order = [b // 1024 for b in COL_BASE]
    w5 = w4[:, :, order]
    return np.ascontiguousarray(w5.transpose(3, 1, 0, 2, 4).reshape(8, 128, NK * 768))


def host_inputs(inputs, S=4096):
    f32 = np.float32
    par = np.concatenate([
        np.asarray(inputs["conv_w"], f32).reshape(3, D),
        np.asarray(inputs["lru_conv_w"], f32).reshape(4, D),
        np.asarray(inputs["lru_conv_b"], f32).reshape(1, D),
        np.asarray(inputs["b_a"], f32).reshape(1, D),
        np.asarray(inputs["b_i"], f32).reshape(1, D),
        np.asarray(inputs["lam"], f32).reshape(1, D),
        np.asarray(inputs["conv_out_g"], f32).reshape(1, D),
        np.asarray(inputs["lru_out_g"], f32).reshape(1, D),
        np.zeros((1, D), f32),
    ], axis=0)
    blk = np.zeros((128, 128), f32)
    blk[:64, :64] = 1.0
    blk[64:, 64:] = 1.0
    shared = {
        "w_in_r": relayout_w_in(np.asarray(inputs["w_in"], f32)),
        "w_out_r": np.ascontiguousarray(np.asarray(inputs["w_out"], f32).reshape(16, 128, D).transpose(1, 0, 2)
                                        .reshape(128, 16 * D)),
        "w_a": np.ascontiguousarray(np.asarray(inputs["w_a"], f32)),
        "w_i": np.ascontiguousarray(np.asarray(inputs["w_i"], f32)),
        "par": np.ascontiguousarray(par),
        "ln_g": np.asarray(inputs["ln_g"], f32).reshape(1, D),
        "final_g": np.asarray(inputs["final_g"], f32).reshape(1, D),
        "ident_bf": np.eye(128, dtype=f32).astype(ml_dtypes.bfloat16),
        "ident_f": np.eye(128, dtype=f32),
        "ones_bf": np.ones((128, 128), f32).astype(ml_dtypes.bfloat16),
        "blk_bf": blk.astype(ml_dtypes.bfloat16),
    }
    return shared


def kernel(**inputs):
    x = np.asarray(inputs["x"], np.float32)
    B, S, _ = x.shape
    shared = host_inputs(inputs)
    nc, es, P = build_program(NT=S // TC)
    in_maps = []
    for b in range(B):
        m = dict(shared)
        m["x"] = np.ascontiguousarray(x[b])
        in_maps.append(m)
    res = run_bass_kernel_spmd(nc, in_maps, core_ids=list(range(B)))
    out = np.stack([np.asarray(r["out"], np.float32) for r in res.results], axis=0)
    return out
```
